# Optimizing a Trainium2 kernel written in Bass

```python
import jax, jax.numpy as jnp
from jax import lax
import numpy as np

D_MODEL = 1024
BATCH = 8
SEQ = 2048
DEPTH = 4
DEC_BATCH = 128
DEC_SEQ = 8
PAST_LEN = 16384
PAGE_SIZE = 128

LRU_WIDTH = D_MODEL
LRU_HEADS = 16
LRU_HEAD_DIM = LRU_WIDTH // LRU_HEADS
CONV_W = 4
LRU_C = 8.0
RWKV_WIDTH = D_MODEL
RWKV_HEAD = 64
RWKV_HEADS = RWKV_WIDTH // RWKV_HEAD
DECAY_RANK = 64
AAA_RANK = 64
GATE_RANK = 128
SHIFT_WIDTH = 3 * RWKV_WIDTH + DECAY_RANK + AAA_RANK + GATE_RANK
IN_COLS = 2 * LRU_WIDTH + SHIFT_WIDTH + 2 * D_MODEL
D_FF = 4 * D_MODEL
DN_ALPHA = (2 * DEPTH) ** 0.25
DN_BETA = (8 * DEPTH) ** -0.25
LN_EPS = 1e-5
GN_EPS = 64e-5

kernel_name = "hawk_rwkv7_parallel_deepnorm_step"


def _layer_norm(x, g, b):
    xf = x.astype(jnp.float32)
    mu = jnp.mean(xf, -1, keepdims=True)
    var = jnp.mean(jnp.square(xf - mu), -1, keepdims=True)
    y = (xf - mu) * lax.rsqrt(var + LN_EPS) * g.astype(jnp.float32) + b.astype(jnp.float32)
    return y.astype(x.dtype)


def _rg_lru(u, h0, pos, wa, ba, wx, bx, a_param):
    bsz, t, c = u.shape
    f32 = jnp.float32
    uf = u.astype(f32)
    uh = uf.reshape(bsz, t, LRU_HEADS, LRU_HEAD_DIM)
    gate_r = jax.nn.sigmoid(jnp.einsum("bthi,hij->bthj", uh, wa.astype(f32)) + ba.astype(f32)).reshape(bsz, t, c)
    gate_i = jax.nn.sigmoid(jnp.einsum("bthi,hij->bthj", uh, wx.astype(f32)) + bx.astype(f32)).reshape(bsz, t, c)
    log_a = -LRU_C * gate_r * jax.nn.softplus(a_param.astype(f32))
    a = jnp.exp(log_a)
    mult = jnp.sqrt(-jnp.expm1(2.0 * log_a))
    mult = jnp.where((pos == 0)[None, :, None], 1.0, mult)
    xin = uf * gate_i * mult

    def combine(left, right):
        a_l, b_l = left
        a_r, b_r = right
        return a_l * a_r, a_r * b_l + b_r

    a_cum, h_from_zero = lax.associative_scan(combine, (a, xin), axis=1)
    h = a_cum * h0.astype(f32)[:, None] + h_from_zero
    return h, h[:, -1]


def _wkv7(r, decay, k, v, a_vec, b_vec, s0):
    def step(s, inp):
        r_t, d_t, k_t, v_t, a_t, b_t = inp
        sa = jnp.einsum("bhvk,bhk->bhv", s, a_t)
        s = s * d_t[:, :, None, :] + sa[..., None] * b_t[:, :, None, :] + v_t[..., None] * k_t[:, :, None, :]
        return s, jnp.einsum("bhvk,bhk->bhv", s, r_t)

    seq = tuple(jnp.swapaxes(z, 0, 1) for z in (r, decay, k, v, a_vec, b_vec))
    s_last, out = lax.scan(step, s0, seq)
    return jnp.swapaxes(out, 0, 1), s_last


def _layer(x, conv_buf, h0, shift_prev, s0, pos, lp):
    bsz, t, _ = x.shape
    dt = x.dtype
    f32 = jnp.float32
    proj = x @ lp["w_in"]
    lru_x, lru_y, shifted, gates = jnp.split(
        proj, [LRU_WIDTH, 2 * LRU_WIDTH, 2 * LRU_WIDTH + SHIFT_WIDTH], axis=-1)

    conv_in = jnp.concatenate([conv_buf.astype(dt), lru_x], axis=1)
    u = lp["conv_b"] + sum(conv_in[:, j:j + t] * lp["conv_w"][j] for j in range(CONV_W))
    new_conv = conv_in[:, -(CONV_W - 1):]
    h, h_last = _rg_lru(u, h0, pos, lp["lru_wa"], lp["lru_ba"], lp["lru_wx"], lp["lru_bx"], lp["lru_a_param"])
    out_a = h.astype(dt) * jax.nn.gelu(lru_y)

    prev = jnp.concatenate([shift_prev[:, None].astype(dt), shifted[:, :-1]], axis=1)
    mixed = shifted + (prev - shifted) * lp["shift_mu"]
    new_shift = shifted[:, -1]
    rw = RWKV_WIDTH
    r, k, v, xw, xa, xg = jnp.split(
        mixed, [rw, 2 * rw, 3 * rw, 3 * rw + DECAY_RANK, 3 * rw + DECAY_RANK + AAA_RANK], axis=-1)
    w_log = -jax.nn.softplus(-(lp["w0"] + jnp.tanh(xw) @ lp["decay_up"]).astype(f32)) - 0.5
    decay = jnp.exp(-jnp.exp(w_log))
    a = jax.nn.sigmoid((lp["a0"] + xa @ lp["aaa_up"]).astype(f32))
    g = jax.nn.sigmoid(xg) @ lp["gate_up"]

    def heads(z):
        return z.astype(f32).reshape(bsz, t, RWKV_HEADS, RWKV_HEAD)

    kk = heads(k * lp["k_k"])
    kk = kk / jnp.maximum(jnp.sqrt(jnp.sum(jnp.square(kk), -1, keepdims=True)), 1e-12)
    a_h = heads(a)
    k_a = lp["k_a"].astype(f32).reshape(RWKV_HEADS, RWKV_HEAD)
    k_h = heads(k) * (1.0 + (a_h - 1.0) * k_a)
    r_h = heads(r)
    v_h = heads(v)
    o, s_last = _wkv7(r_h, heads(decay), k_h, v_h, -kk, kk * a_h, s0.astype(f32))
    o_mu = jnp.mean(o, -1, keepdims=True)
    o_var = jnp.mean(jnp.square(o - o_mu), -1, keepdims=True)
    o_n = ((o - o_mu) * lax.rsqrt(o_var + GN_EPS)).reshape(bsz, t, rw)
    o_n = o_n * lp["gn_w"].astype(f32) + lp["gn_b"].astype(f32)
    bonus = jnp.sum(r_h * k_h * lp["r_k"].astype(f32), -1, keepdims=True) * v_h
    out_b = ((o_n + bonus.reshape(bsz, t, rw)) * g.astype(f32)).astype(dt)

    gate_a, gate_b = jnp.split(gates, 2, axis=-1)
    merged = jax.nn.sigmoid(gate_a) * out_a + jax.nn.sigmoid(gate_b) * out_b
    x = _layer_norm(DN_ALPHA * x + merged @ lp["w_out"], lp["ln1_g"], lp["ln1_b"])

    hid = jnp.square(jax.nn.relu(x @ lp["mlp_w1"]))
    x = _layer_norm(DN_ALPHA * x + hid @ lp["mlp_w2"], lp["ln2_g"], lp["ln2_b"])
    return x, new_conv, h_last, new_shift, s_last


def setup_inputs(seed: int = 0) -> dict:
    key = jax.random.key(seed)
    ks = jax.random.split(key, 40)

    def nrm(k, shape, scale):
        return jax.random.normal(k, shape, jnp.float32) * scale

    L = DEPTH
    unif = jax.random.uniform(ks[10], (L, LRU_WIDTH), jnp.float32, 0.9 ** 2, 0.999 ** 2)
    a_real = 0.5 * jnp.log(unif)
    lru_a_param = jnp.log(jnp.expm1(-a_real))
    return {
        "x_prompt": nrm(ks[0], (BATCH, SEQ, D_MODEL), 1.0),
        "x_sample": nrm(ks[1], (DEC_BATCH, DEC_SEQ, D_MODEL), 1.0),
        "state_conv": nrm(ks[2], (L, DEC_BATCH, CONV_W - 1, LRU_WIDTH), 1.0),
        "state_lru": nrm(ks[3], (L, DEC_BATCH, LRU_WIDTH), 0.5),
        "state_shift": nrm(ks[4], (L, DEC_BATCH, SHIFT_WIDTH), 1.0),
        "state_wkv": nrm(ks[5], (L, DEC_BATCH, RWKV_HEADS, RWKV_HEAD, RWKV_HEAD), 0.1),
        "w_in": nrm(ks[6], (L, D_MODEL, IN_COLS), D_MODEL ** -0.5),
        "conv_w": nrm(ks[7], (L, CONV_W, LRU_WIDTH), 0.5),
        "conv_b": nrm(ks[8], (L, LRU_WIDTH), 0.01),
        "lru_wa": nrm(ks[9], (L, LRU_HEADS, LRU_HEAD_DIM, LRU_HEAD_DIM), LRU_HEAD_DIM ** -0.5),
        "lru_ba": nrm(ks[11], (L, LRU_HEADS, LRU_HEAD_DIM), 0.01),
        "lru_wx": nrm(ks[12], (L, LRU_HEADS, LRU_HEAD_DIM, LRU_HEAD_DIM), LRU_HEAD_DIM ** -0.5),
        "lru_bx": nrm(ks[13], (L, LRU_HEADS, LRU_HEAD_DIM), 0.01),
        "lru_a_param": lru_a_param,
        "shift_mu": jax.random.uniform(ks[14], (L, SHIFT_WIDTH), jnp.float32),
        "decay_up": nrm(ks[15], (L, DECAY_RANK, RWKV_WIDTH), 0.5 * DECAY_RANK ** -0.5),
        "w0": jax.random.uniform(ks[16], (L, RWKV_WIDTH), jnp.float32, -3.0, 1.0),
        "aaa_up": nrm(ks[17], (L, AAA_RANK, RWKV_WIDTH), 0.5 * AAA_RANK ** -0.5),
        "a0": nrm(ks[18], (L, RWKV_WIDTH), 0.1),
        "gate_up": nrm(ks[19], (L, GATE_RANK, RWKV_WIDTH), GATE_RANK ** -0.5),
        "k_k": 0.85 + nrm(ks[20], (L, RWKV_WIDTH), 0.02),
        "k_a": 1.0 + nrm(ks[21], (L, RWKV_WIDTH), 0.02),
        "r_k": nrm(ks[22], (L, RWKV_HEADS, RWKV_HEAD), 0.1),
        "gn_w": 1.0 + nrm(ks[23], (L, RWKV_WIDTH), 0.02),
        "gn_b": nrm(ks[24], (L, RWKV_WIDTH), 0.01),
        "w_out": nrm(ks[25], (L, D_MODEL, D_MODEL), DN_BETA * D_MODEL ** -0.5),
        "ln1_g": 1.0 + nrm(ks[26], (L, D_MODEL), 0.02),
        "ln1_b": nrm(ks[27], (L, D_MODEL), 0.01),
        "mlp_w1": nrm(ks[28], (L, D_MODEL, D_FF), D_MODEL ** -0.5),
        "mlp_w2": nrm(ks[29], (L, D_FF, D_MODEL), DN_BETA * D_FF ** -0.5),
        "ln2_g": 1.0 + nrm(ks[30], (L, D_MODEL), 0.02),
        "ln2_b": nrm(ks[31], (L, D_MODEL), 0.01),
    }


def reference(x_prompt, x_sample, state_conv, state_lru, state_shift, state_wkv,
              w_in, conv_w, conv_b, lru_wa, lru_ba, lru_wx, lru_bx, lru_a_param,
              shift_mu, decay_up, w0, aaa_up, a0, gate_up, k_k, k_a, r_k, gn_w, gn_b,
              w_out, ln1_g, ln1_b, mlp_w1, mlp_w2, ln2_g, ln2_b):
    bp, tp, _ = x_prompt.shape
    ts = x_sample.shape[1]
    pos_p = jnp.arange(tp)
    pos_s = PAST_LEN + jnp.arange(ts)
    dt = x_prompt.dtype
    zc = jnp.zeros((bp, CONV_W - 1, LRU_WIDTH), dt)
    zh = jnp.zeros((bp, LRU_WIDTH), jnp.float32)
    zsh = jnp.zeros((bp, SHIFT_WIDTH), dt)
    zs = jnp.zeros((bp, RWKV_HEADS, RWKV_HEAD, RWKV_HEAD), jnp.float32)

    xp, xs = x_prompt, x_sample
    conv_p, lru_p, shift_p, wkv_p = [], [], [], []
    conv_s, lru_s, shift_s, wkv_s = [], [], [], []
    for l in range(DEPTH):
        lp = {
            "w_in": w_in[l], "conv_w": conv_w[l], "conv_b": conv_b[l],
            "lru_wa": lru_wa[l], "lru_ba": lru_ba[l], "lru_wx": lru_wx[l], "lru_bx": lru_bx[l],
            "lru_a_param": lru_a_param[l], "shift_mu": shift_mu[l],
            "decay_up": decay_up[l], "w0": w0[l], "aaa_up": aaa_up[l], "a0": a0[l],
            "gate_up": gate_up[l], "k_k": k_k[l], "k_a": k_a[l], "r_k": r_k[l],
            "gn_w": gn_w[l], "gn_b": gn_b[l], "w_out": w_out[l],
            "ln1_g": ln1_g[l], "ln1_b": ln1_b[l], "mlp_w1": mlp_w1[l], "mlp_w2": mlp_w2[l],
            "ln2_g": ln2_g[l], "ln2_b": ln2_b[l],
        }
        xp, c, h, sh, s = _layer(xp, zc, zh, zsh, zs, pos_p, lp)
        conv_p.append(c); lru_p.append(h); shift_p.append(sh); wkv_p.append(s)
        xs, c, h, sh, s = _layer(xs, state_conv[l], state_lru[l], state_shift[l], state_wkv[l], pos_s, lp)
        conv_s.append(c); lru_s.append(h); shift_s.append(sh); wkv_s.append(s)

    new_conv_p = jnp.stack(conv_p).astype(state_conv.dtype)
    new_lru_p = jnp.stack(lru_p).astype(state_lru.dtype)
    new_shift_p = jnp.stack(shift_p).astype(state_shift.dtype)
    new_wkv_p = jnp.stack(wkv_p).astype(state_wkv.dtype)
    new_conv_s = jnp.stack(conv_s).astype(state_conv.dtype)
    new_lru_s = jnp.stack(lru_s).astype(state_lru.dtype)
    new_shift_s = jnp.stack(shift_s).astype(state_shift.dtype)
    new_wkv_s = jnp.stack(wkv_s).astype(state_wkv.dtype)
    return (xp, xs, new_conv_p, new_lru_p, new_shift_p, new_wkv_p,
            new_conv_s, new_lru_s, new_shift_s, new_wkv_s)
```

```python
import heapq
import numpy as np
import concourse.bass as bass
import concourse.mybir as mybir

F32 = mybir.dt.float32
BF16 = mybir.dt.bfloat16
AF = mybir.ActivationFunctionType
ALU = mybir.AluOpType

ENGS = ("pe", "act", "dve", "pool", "sp")
N_DMA_SEMS = 12
EPOCH = 30000
SCHEDULE = True
SEM_LAT = 0.25
DEBUG_LINES = False
DBG_FREE = [0, 0]
CHAIN = {}


def region_of(ap):
    pat = ap.ap
    pitch, npart = pat[0]
    off = int(ap.offset)
    if pitch == 0:
        pitch = 1 << 40
    p_lo = off // pitch
    f_lo = off % pitch
    span = 0
    for step, cnt in pat[1:]:
        span += (cnt - 1) * abs(step)
    return (ap.name, p_lo, p_lo + npart, f_lo, f_lo + span + 1)


def _overlap(a, b):
    return a[1] < b[2] and b[1] < a[2] and a[3] < b[4] and b[3] < a[4]


def _covers(a, b):
    return a[1] <= b[1] and a[2] >= b[2] and a[3] <= b[3] and a[4] >= b[4]


def _onchip(ap):
    return ap is not None and str(getattr(ap, "space", "")) in ("SB", "PSUM")


class Op:
    __slots__ = ("id", "eng", "fn", "preds", "nsucc", "succs", "is_dma", "occ", "lat", "prio", "line",
                 "start", "fin", "sem", "val", "waits", "npend", "is_out", "mode")


class Prog:
    def __init__(self, nc):
        self.nc = nc
        self.all = []
        self.live = {}
        self.liver = {}
        self.pe_bank = {}

    def _track(self, o, reads, writes):
        preds = set()
        for ap in reads:
            if not _onchip(ap):
                continue
            r = region_of(ap)
            for ent in self.live.get(r[0], ()):
                if ent[1] == "w" and _overlap(ent[0], r):
                    preds.add(ent[2])
        for ap in writes:
            if not _onchip(ap):
                continue
            r = region_of(ap)
            for ent in self.live.get(r[0], ()):
                if _overlap(ent[0], r):
                    preds.add(ent[2])
            for ent in self.liver.get(r[0], ()):
                if _overlap(ent[0], r):
                    preds.add(ent[2])
            if o.eng == "pe":
                key = (r[0], r[3] // (512 if ap.dtype == F32 else 1024))
                prev = self.pe_bank.get(key)
                if prev is not None:
                    preds.add(prev)
                self.pe_bank[key] = o
        ch = CHAIN.get(o.eng)
        if ch is not None:
            if ch:
                if not (o.eng == "dve" and DBG_FREE[0] <= o.id < DBG_FREE[1]):
                    preds.add(ch[0])
                ch[0] = o
            else:
                ch.append(o)
        preds.discard(o)
        o.preds = preds
        for ap in reads:
            if not _onchip(ap):
                continue
            r = region_of(ap)
            self.liver.setdefault(r[0], []).append([r, "r", o])
        for ap in writes:
            if not _onchip(ap):
                continue
            r = region_of(ap)
            lst = self.live.setdefault(r[0], [])
            lst[:] = [ent for ent in lst if not _covers(r, ent[0])]
            lst.append([r, "w", o])
            lr = self.liver.get(r[0])
            if lr:
                lr[:] = [ent for ent in lr if not _covers(r, ent[0])]

    def op(self, eng, fn, reads=(), writes=(), cost=0.3, mode=0):
        o = Op()
        o.id = len(self.all)
        o.eng, o.fn, o.is_dma, o.is_out = eng, fn, False, False
        o.mode = mode
        o.occ = cost
        o.lat = cost + (0.1 if eng == "pe" else 0.15)
        if DEBUG_LINES:
            import sys as _s
            o.line = _s._getframe(2).f_lineno
        self._track(o, reads, writes)
        self.all.append(o)
        return o

    def dma(self, eng, out, in_, is_output=False, nbytes=0):
        o = Op()
        o.id = len(self.all)
        o.eng, o.is_dma, o.is_out = eng, True, is_output
        o.mode = 0
        o.fn = lambda e, out=out, in_=in_: e.dma_start(out=out, in_=in_)
        if not nbytes:
            nbytes = 4
            for d in out.shape:
                nbytes *= d
        if DEBUG_LINES:
            import sys as _s
            o.line = _s._getframe(1).f_lineno
        o.occ = 1.5 if eng == "pool" else 0.15
        o.lat = o.occ + 2.0 + nbytes / 100e3
        self._track(o, [in_], [out])
        self.all.append(o)
        return o

    def schedule(self):
        ops = self.all
        for o in ops:
            o.succs = []
        for o in ops:
            for p in o.preds:
                p.succs.append(o)
        for o in reversed(ops):
            m = 0.0
            for s in o.succs:
                if s.prio > m:
                    m = s.prio
            o.prio = m + o.lat
        order = {e: [] for e in ENGS}
        if not SCHEDULE:
            for o in ops:
                order[o.eng].append(o)
            return order
        cursor = {e: 0.0 for e in ENGS}
        avail = {e: [] for e in ENGS}
        ready_t = {}
        for o in ops:
            o.npend = len(o.preds)
            if o.npend == 0:
                ready_t[o.id] = 0.0
                heapq.heappush(avail[o.eng], (-o.prio, o.id, o))
        nleft = len(ops)
        while nleft:
            best_e = None
            for e in ENGS:
                if avail[e] and (best_e is None or cursor[e] < cursor[best_e]):
                    best_e = e
            e = best_e
            t = cursor[e]
            h = avail[e]
            pick = None
            popped = []
            earliest = None
            for _ in range(min(len(h), 24)):
                item = heapq.heappop(h)
                popped.append(item)
                rt = ready_t[item[1]]
                if rt <= t + 1e-9:
                    pick = item
                    break
                if earliest is None or rt < ready_t[earliest[1]]:
                    earliest = item
            if pick is None:
                pick = earliest
            for item in popped:
                if item is not pick:
                    heapq.heappush(h, item)
            o = pick[2]
            st = max(t, ready_t[o.id])
            o.start = st
            o.fin = st + o.lat
            cursor[e] = st + o.occ
            order[e].append(o)
            nleft -= 1
            for s in o.succs:
                s.npend -= 1
                if s.npend == 0:
                    rt = 0.0
                    for p in s.preds:
                        f = p.fin + (SEM_LAT if p.eng != s.eng else 0.0)
                        if f > rt:
                            rt = f
                    ready_t[s.id] = rt
                    heapq.heappush(avail[s.eng], (-s.prio, s.id, s))
        self.sim_time = max(o.fin for o in ops)
        return order

    def emit(self):
        nc = self.nc
        from contextlib import ExitStack

        order = self.schedule()
        self.order = order
        nep = {}
        for e in ENGS:
            cnt, ep = 0, 0
            uses = [0] * N_DMA_SEMS
            rr = 0
            for o in order[e]:
                if o.is_dma:
                    k = rr
                    rr = (rr + 1) % N_DMA_SEMS
                    o.waits = {}
                    if uses[k] > 0:
                        o.waits[("D", e, k)] = 16 * uses[k]
                    uses[k] += 1
                    o.sem, o.val = ("D", e, k), 16 * uses[k]
                else:
                    cnt += 1
                    if cnt > EPOCH:
                        cnt, ep = 1, ep + 1
                    o.sem, o.val = ("E", e, ep), cnt
                    o.waits = {}
            nep[e] = ep + 1
        for e in ENGS:
            waited = {}
            prev = None
            for o in order[e]:
                w = o.waits
                if e == "pe":
                    if prev is not None and (prev.mode != o.mode or o.mode == 2):
                        w[prev.sem] = max(w.get(prev.sem, 0), prev.val)
                    prev = o
                for p in o.preds:
                    if p.eng == "pe" and e == "pe":
                        continue
                    if w.get(p.sem, 0) < p.val:
                        w[p.sem] = p.val
                o.waits = []
                for k, v in sorted(w.items()):
                    if waited.get(k, 0) < v:
                        waited[k] = v
                        o.waits.append((k, v))

        with ExitStack() as st:
            sems = {}
            for e in ENGS:
                for ep in range(nep[e]):
                    sems[("E", e, ep)] = st.enter_context(nc.semaphore("sem_%s_%d" % (e, ep)))
            for e in ("sp", "pool"):
                for k in range(N_DMA_SEMS):
                    sems[("D", e, k)] = st.enter_context(nc.semaphore("semd_%s_%d" % (e, k)))
            block = st.enter_context(nc.Block())

            def run(engname, e):
                for o in order[engname]:
                    for k, v in o.waits:
                        e.wait_ge(sems[k], v)
                    ins = o.fn(e)
                    ins.then_inc(sems[o.sem], 16 if o.is_dma else 1)
                if engname == "sp":
                    final = {}
                    for o in self.all:
                        if o.is_out:
                            final[o.sem] = max(final.get(o.sem, 0), o.val)
                    for k, v in final.items():
                        e.wait_ge(sems[k], v)

            @block.tensor
            def _(e):
                run("pe", e)

            @block.scalar
            def _(e):
                run("act", e)

            @block.vector
            def _(e):
                run("dve", e)

            @block.gpsimd
            def _(e):
                run("pool", e)

            @block.sync
            def _(e):
                run("sp", e)

from concourse.bass_utils import run_bass_kernel_spmd

from contextlib import ExitStack

L = 4
NL_RUN = 4
DBG_NC = 8
DBG_NB = None
DBG_POST = True
D = 1024
NTOK = 2176
NPR = 2048
NSQ = 16
TS = 8
WB = 128
NV = 29
ALPHA = (2 * L) ** 0.25
C0 = 0.6065306597126334
LN_EPS = 1e-5
GN_EPS = 64e-5
V_CW, V_CB, V_BA, V_BX, V_AP, V_W0, V_A0, V_KK, V_KA, V_RK, V_GW, V_GB = 0, 4, 5, 6, 7, 8, 9, 10, 11, 12, 13, 14
V_L1G, V_L1B, V_L2G, V_L2B, V_MUR, V_MUL, V_CP, V_OMKA = 15, 16, 17, 18, 19, 22, 23, 24
V_NBA, V_NBX, V_NW0, V_NA0 = 25, 26, 27, 28

C_BONE, C_BONE64, C_ONESD, C_RMP, C_RMS = [i * 128 for i in range(5)]
NCF = 5 * 128
C_ID, C_SU, C_UI, C_SL, C_SSU, C_SUI, C_SSL = [i * 128 for i in range(7)]
C_SMF = 7 * 128
C_SMT = C_SMF + 16 * 128
NCB = C_SMT + 16


def make_consts():
    c = np.zeros((128, NCF), np.float32)
    d = np.zeros((128, NCB), np.float32)
    i = np.arange(128)
    s, t = i[:, None], i[None, :]
    d[:, C_ID:C_ID + 128] = (s == t)
    d[:, C_SU:C_SU + 128] = (s < t)
    d[:, C_UI:C_UI + 128] = (s <= t)
    d[:, C_SL:C_SL + 128] = (s > t)
    same = (s // 8) == (t // 8)
    d[:, C_SSU:C_SSU + 128] = (s < t) & same
    d[:, C_SUI:C_SUI + 128] = (s <= t) & same
    d[:, C_SSL:C_SSL + 128] = (s > t) & same
    c[:, C_BONE:C_BONE + 128] = ((s // 64) == (t // 64))
    c[:, C_BONE64:C_BONE64 + 128] = ((s // 64) == (t // 64)) / 64.0
    c[:, C_ONESD:C_ONESD + 128] = 1.0 / D
    c[:, C_RMP:C_RMP + 128] = (t % 128 != 0)
    c[:, C_RMS:C_RMS + 128] = (t % 8 != 0)
    for j in range(16):
        d[:, C_SMF + j * 128:C_SMF + (j + 1) * 128] = ((t // 8) == j)
        d[:, C_SMT + j] = ((i // 8) == j)
    return c, d


def build_program():
    nc = bass.Bass("TRN2", target_bir_lowering=False)

    def din(name, shape):
        return nc.dram_tensor(name, shape, F32, kind="ExternalInput").ap()

    def dout(name, shape):
        return nc.dram_tensor(name, shape, F32, kind="ExternalOutput").ap()

    xT = din("xT", [128, 8, NTOK])
    vecs = din("vecs", [128, L * NV * 8])
    cst = din("cst", [128, NCF])
    cstb = din("cstb", [128, NCB])
    wc = din("wc", [L, 8, 128, 8 * 896])
    wl = din("wl", [L, 128, 8 * 256])
    wo = din("wo", [L, 8, 128, 1024])
    smallw = din("smallw", [L, 8, 128, 512])
    w1 = din("w1", [L, 8, 128, 8 * 512])
    w2 = din("w2", [L, 8, 128, 4 * 1024])
    sconv = din("sconv", [L, 8, 128, 48])
    slru = din("slru", [L, 128, 128])
    sshift = din("sshift", [L, 128, 26 * 16])
    swkv = din("swkv", [L, 8, 128, 1024])
    yT = dout("yT", [128, 8, NTOK])
    o_conv_p = dout("o_conv_p", [L, 8, 128, 3])
    o_lru_p = dout("o_lru_p", [L, 128, 8])
    o_shift_p = dout("o_shift_p", [L, 128, 26])
    o_wkv_p = dout("o_wkv_p", [L, 8, 128, 64])
    o_conv_s = dout("o_conv_s", [L, 8, 128, 48])
    o_lru_s = dout("o_lru_s", [L, 128, 128])
    o_shift_s = dout("o_shift_s", [L, 128, 26 * 16])
    o_wkv_s = dout("o_wkv_s", [L, 8, 128, 1024])

    st = ExitStack()
    P = Prog(nc)
    tot = [0]

    def sb(name, cols, dt=F32):
        tot[0] += cols * (4 if dt == F32 else 2)
        return st.enter_context(nc.sbuf_tensor(name, [128, cols], dt))

    xf = st.enter_context(nc.sbuf_tensor("xf", [128, 8, NTOK], F32)); tot[0] += 8 * NTOK * 4
    xb = st.enter_context(nc.sbuf_tensor("xb", [128, 8, NTOK], BF16)); tot[0] += 8 * NTOK * 2
    vec = sb("vec", L * NV * 8)
    cf = sb("cf", NCF)
    cb = sb("cb", NCB, BF16)
    loraA = sb("loraA", NTOK, BF16)
    loraB = sb("loraB", NTOK, BF16)
    smw = [sb("smw%d" % i, 512, BF16) for i in range(2)]
    wob = [sb("wob%d" % i, 1024, BF16) for i in range(2)]
    AR = sb("AR", 16384, BF16)
    wcb = [AR[:, i * 7168:(i + 1) * 7168] for i in range(2)]
    wlb = AR[:, 14336:16384]
    w1b = [AR[:, i * 8192:i * 8192 + 4096] for i in range(2)]
    w2b = [AR[:, i * 8192 + 4096:(i + 1) * 8192] for i in range(2)]
    ps = st.enter_context(nc.psum_tensor("ps", [128, 7, 512], F32))
    pst = st.enter_context(nc.psum_tensor("pst", [128, 1024], BF16))
    psrr = [0]

    role = ["all"]
    rr_a = [0]
    rr_b = [0]

    def bank():
        if role[0] == "a":
            b = rr_a[0]
            rr_a[0] = (b + 1) % 3
        elif role[0] == "b":
            b = 3 + rr_b[0]
            rr_b[0] = (rr_b[0] + 1) % 4
        else:
            b = psrr[0]
            psrr[0] = (b + 1) % 7
        return ps[:, b, :]

    def nfree(ap):
        n = 1
        for d in ap.shape[1:]:
            n *= d
        return n

    def inps(ap):
        return str(ap.space) == "PSUM"

    def mm(out, lhsT, rhs, start=True, stop=True):
        c = (64 + nfree(rhs)) / 1400.0 * (4 if lhsT.dtype == F32 else 1) + 0.02
        P.op("pe", lambda e: e.matmul(out, lhsT, rhs, start=start, stop=stop), reads=[lhsT, rhs], writes=[out], cost=c,
             mode=(1 if lhsT.dtype == F32 else (0 if lhsT.shape[0] == 128 else 10 + region_of(lhsT)[1] // 32)))

    def tr(out, in_):
        ident = cb[0:in_.shape[0], C_ID:C_ID + in_.shape[0]]
        P.op("pe", lambda e: e.transpose(out, in_, ident), reads=[in_, ident], writes=[out], cost=0.15, mode=2)

    def act(out, in_, func, bias=None, scale=None):
        kw = {}
        rd = [in_]
        c = 0.27 + nfree(in_) / 1400.0
        if bias is not None:
            kw["bias"] = bias
            if not isinstance(bias, (int, float)):
                rd.append(bias)
                c += 0.09
        if scale is not None:
            kw["scale"] = scale
            if not isinstance(scale, (int, float)):
                rd.append(scale)
                c += 0.09
        P.op("act", lambda e: e.activation(out=out, in_=in_, func=func, **kw), reads=rd, writes=[out], cost=c)

    def ecost(eng, n, two=False):
        if eng == "pool":
            return 0.15 + n / 350.0
        return 0.2 + n * (2 if two else 1) / 960.0

    def tt(eng, out, in0, in1, op):
        two = not (inps(in0) or inps(in1))
        P.op(eng, lambda e: e.tensor_tensor(out=out, in0=in0, in1=in1, op=op), reads=[in0, in1], writes=[out], cost=ecost(eng, nfree(out), two))

    def ts(eng, out, in0, s1, op0, s2=None, op1=None):
        rd = [in0] + [s for s in (s1, s2) if s is not None and not isinstance(s, (int, float))]
        c = ecost(eng, nfree(out))
        if op1 is None:
            P.op(eng, lambda e: e.tensor_scalar(out=out, in0=in0, scalar1=s1, scalar2=None, op0=op0), reads=rd, writes=[out], cost=c)
        else:
            P.op(eng, lambda e: e.tensor_scalar(out=out, in0=in0, scalar1=s1, scalar2=s2, op0=op0, op1=op1), reads=rd, writes=[out], cost=c)

    def stt(out, in0, scalar, in1, op0, op1):
        rd = [in0, in1] + ([] if isinstance(scalar, (int, float)) else [scalar])
        P.op("dve", lambda e: e.scalar_tensor_tensor(out=out, in0=in0, scalar=scalar, in1=in1, op0=op0, op1=op1), reads=rd, writes=[out], cost=ecost("dve", nfree(out), True))

    def scan(out, d0, d1, init):
        rd = [d0, d1] + ([] if isinstance(init, (int, float)) else [init])
        P.op("dve", lambda e: e.tensor_tensor_scan(out=out, data0=d0, data1=d1, initial=init, op0=ALU.mult, op1=ALU.add), reads=rd, writes=[out], cost=ecost("dve", nfree(out), True))

    def cp(eng, out, in_):
        if eng == "act":
            P.op("act", lambda e: e.copy(out=out, in_=in_), reads=[in_], writes=[out], cost=0.22 + nfree(out) / 1400.0)
        else:
            P.op(eng, lambda e: e.tensor_copy(out=out, in_=in_), reads=[in_], writes=[out], cost=ecost(eng, nfree(out)))

    def mset(eng, ap, val):
        P.op(eng, lambda e: e.memset(ap, val), reads=[], writes=[ap], cost=ecost(eng, nfree(ap)))

    def recip(out, in_):
        P.op("dve", lambda e: e.reciprocal(out=out, in_=in_), reads=[in_], writes=[out], cost=ecost("dve", nfree(out)))

    def sigm(out, in_, nbias=None):
        if nbias is None:
            act(out, in_, AF.Exp, scale=-1.0)
        else:
            act(out, in_, AF.Exp, scale=-1.0, bias=nbias)
        ts("pool", out, out, 1.0, ALU.add)
        recip(out, out)

    def VV(l, v, c):
        o = (l * NV + v) * 8 + c
        return vec[:, o:o + 1]

    P.dma("sp", cf[:], cst)
    P.dma("sp", vec[:], vecs)
    P.dma("pool", cb[:], cstb)
    for kc in range(8):
        P.dma("sp", xf[:, kc, :], xT[:, kc, :])
        P.dma("pool", xb[:, kc, :], xT[:, kc, :])
    vec4 = vec[:].rearrange("p (l v c) -> p l v c", l=L, v=NV)
    for l in range(L):
        act(vec4[:, l, V_CP, :], vec4[:, l, V_AP, :], AF.Exp)
        act(vec4[:, l, V_CP, :], vec4[:, l, V_CP, :], AF.Ln, bias=1.0)
        ts("dve", vec4[:, l, V_CP, :], vec4[:, l, V_CP, :], -8.0, ALU.mult)
        ts("dve", vec4[:, l, V_OMKA, :], vec4[:, l, V_KA, :], -1.0, ALU.mult, 1.0, ALU.add)
        for vs, vd in ((V_BA, V_NBA), (V_BX, V_NBX), (V_W0, V_NW0), (V_A0, V_NA0)):
            ts("dve", vec4[:, l, vd, :], vec4[:, l, vs, :], -1.0, ALU.mult)

    W = WB
    lx = sb("lx", 3 + W)
    lxs = sb("lxs", 16 * 11)
    u = sb("u", W); ubf = sb("ubf", W, BF16)
    gr = sb("gr", W); gi = sb("gi", W); mlt = sb("mlt", W); hh = sb("hh", W); hc = sb("hc", 1)
    h0s = sb("h0s", 16)
    gy = sb("gy", W); sga = sb("sga", W); oa = sb("oa", W)
    shp = [sb("shp%d" % q, 1 + W) for q in range(3)]
    shs = [sb("shs%d" % q, 16 * 9) for q in range(3)]
    shlp = shp
    shls = shs
    tmp = sb("tmp", W)
    mx = [sb("mx%d" % q, W) for q in range(3)]
    vbf = sb("vbf", W, BF16)
    sg = sb("sg", W); asig = sb("asig", W); gg = sb("gg", W)
    UF = sb("UF", 1024)
    kk, kk2, rn, bb, kh, cum, gam, ig = [UF[:, i * 128:(i + 1) * 128] for i in range(8)]
    e0 = sb("e0", W)
    Rt = sb("Rt", W, BF16); At = sb("At", W, BF16); Kt = sb("Kt", W, BF16); Bt = sb("Bt", W, BF16)
    Bh = sb("Bh", 128, BF16); Kh = sb("Kh", 128, BF16)
    rk = sb("rk", W)
    TT = sb("TT", 384, BF16)
    MX = sb("MX", 512, BF16); MY = sb("MY", 512, BF16); MZ = sb("MZ", 256, BF16)
    PP = [sb("PP%d" % i, 512, BF16) for i in range(2)]
    NN = [sb("NN%d" % i, 256, BF16) for i in range(2)]
    XTs = sb("XTs", 128, BF16); UTs = sb("UTs", 128, BF16)
    Sp = sb("Sp", 64); Spb = sb("Spb", 64, BF16)
    Ss = sb("Ss", 1024); Ssb = sb("Ssb", 1024, BF16)
    UB = sb("UB", 4096, BF16)
    Atj, Rtj, BhTj, KhTj = [UB[:, i * 1024:(i + 1) * 1024] for i in range(4)]
    WK0 = dict(MX=MX, MY=MY, MZ=MZ, PP=PP, NN=NN, TT=TT, XTs=XTs, UTs=UTs, Bh=Bh, Kh=Kh)
    WK1 = dict(MX=UB[:, 0:512], MY=UB[:, 512:1024], MZ=UB[:, 1024:1280],
               PP=[UB[:, 1280:1792], UB[:, 1792:2304]], NN=[UB[:, 2304:2560], UB[:, 2560:2816]],
               TT=UB[:, 2816:3200], XTs=UB[:, 3200:3328], UTs=UB[:, 3328:3456],
               Bh=UB[:, 3456:3584], Kh=UB[:, 3584:3712])
    of_ = sb("of", W); o2 = sb("o2", W); mus = sb("mus", W); var = sb("var", W)
    mrg = sb("mrg", W, BF16)
    st_shift_p = sb("st_shift_p", 26)
    st_lru_p = sb("st_lru_p", 8)
    scv = sb("scv", 48); scvo = sb("scvo", 48); sshl = sb("sshl", 416); slrl = sb("slrl", 128)
    st_shift_s = sshl
    st_lru_s = slrl
    oa2 = [oa, sb("oa_b", W)]; rk2 = [rk, sb("rk_b", W)]; gg2 = [gg, sb("gg_b", W)]
    sgb = sb("sgb", W); tmp2 = tmp
    gendc = sb("gendc", 2); gends = sb("gends", 16)
    lnq, lnm, lnr, hr = [UF[:, i * 256:(i + 1) * 256] for i in range(4)]
    hid = [UB[:, j * 256:(j + 1) * 256] for j in range(4)]
    print("SBUF bytes/partition:", tot[0])

    blocks = [(t0, W, "p") for t0 in range(0, NPR, W)] + [(NPR, 128, "s")]
    nblk = len(blocks)

    def v3(ap, ns):
        return ap.rearrange("p (s t) -> p s t", s=ns)

    def mixer_block(l, c, bi):
        t0, Wd, kind = blocks[bi]
        ns = 1 if kind == "p" else NSQ
        T = Wd // ns
        nch = Wd // 128
        last_p = (kind == "p" and bi == nblk - 2)
        wcc = wcb[c % 2]
        xbs = lambda kc: xb[:, kc, t0:t0 + Wd]
        oa, rk, gg = oa2[bi % 2], rk2[bi % 2], gg2[bi % 2]
        WK = WK1 if (kind == "p" and bi % 2 == 1) else WK0
        MX, MY, MZ, PP, NN, TT, XTs, UTs, Bh, Kh = (WK[k] for k in ("MX", "MY", "MZ", "PP", "NN", "TT", "XTs", "UTs", "Bh", "Kh"))

        def proj(j):
            b = bank()
            for kc in range(8):
                mm(b[:, 0:Wd], wcc[:, kc * 896 + j * 128:kc * 896 + (j + 1) * 128], xbs(kc), start=(kc == 0), stop=(kc == 7))
            return b[:, 0:Wd]

        role[0] = "a"
        p_lx = proj(0)
        if kind == "p":
            L3 = v3(lx[:, 0:3 + T], 1)
            if bi == 0:
                mset("pool", lx[:, 0:3], 0.0)
            else:
                Tp = blocks[bi - 1][1]
                cp("pool", tmp[:, 0:3], lx[:, Tp:Tp + 3])
                cp("pool", lx[:, 0:3], tmp[:, 0:3])
        else:
            L3 = v3(lxs[:, :], NSQ)
        cp("act", L3[:, :, 3:3 + T], v3(p_lx, ns))
        u3 = v3(u[:, 0:Wd], ns)
        ts("dve", u3, L3[:, :, 0:T], VV(l, V_CW + 0, c), ALU.mult, VV(l, V_CB, c), ALU.add)
        for j in range(1, 4):
            stt(u3, L3[:, :, j:j + T], VV(l, V_CW + j, c), u3, ALU.mult, ALU.add)
        if last_p:
            P.dma("sp", o_conv_p[l, c], lx[:, T:T + 3], is_output=True)
        if kind == "s":
            cp("pool", v3(scvo[:, :], NSQ), L3[:, :, 8:11])
            P.dma("sp", o_conv_s[l, c], scvo[:, :], is_output=True)
        cp("pool", ubf[:, 0:Wd], u[:, 0:Wd])
        p_gr = bank()[:, 0:Wd]
        smc = smw[c % 2]
        mm(p_gr, smc[:, 256:384], ubf[:, 0:Wd])
        p_gi = bank()[:, 0:Wd]
        mm(p_gi, smc[:, 384:512], ubf[:, 0:Wd])
        sigm(gr[:, 0:Wd], p_gr, VV(l, V_NBA, c))
        sigm(gi[:, 0:Wd], p_gi, VV(l, V_NBX, c))
        act(gr[:, 0:Wd], gr[:, 0:Wd], AF.Exp, scale=VV(l, V_CP, c))
        tt("pool", mlt[:, 0:Wd], gr[:, 0:Wd], gr[:, 0:Wd], ALU.mult)
        ts("pool", mlt[:, 0:Wd], mlt[:, 0:Wd], -1.0, ALU.mult, 1.0, ALU.add)
        act(mlt[:, 0:Wd], mlt[:, 0:Wd], AF.Ln)
        act(mlt[:, 0:Wd], mlt[:, 0:Wd], AF.Exp, scale=0.5)
        if kind == "p" and bi == 0:
            mset("pool", mlt[:, 0:1], 1.0)
        tt("dve", gi[:, 0:Wd], gi[:, 0:Wd], u[:, 0:Wd], ALU.mult)
        tt("dve", gi[:, 0:Wd], gi[:, 0:Wd], mlt[:, 0:Wd], ALU.mult)
        if kind == "p":
            if bi == 0:
                scan(hh[:, 0:Wd], gr[:, 0:Wd], gi[:, 0:Wd], 0.0)
            else:
                scan(hh[:, 0:Wd], gr[:, 0:Wd], gi[:, 0:Wd], hc[:, 0:1])
            cp("pool", hc[:, 0:1], hh[:, Wd - 1:Wd])
            if last_p:
                cp("pool", st_lru_p[:, c:c + 1], hh[:, Wd - 1:Wd])
        else:
            for j in range(NSQ):
                scan(hh[:, j * 8:(j + 1) * 8], gr[:, j * 8:(j + 1) * 8], gi[:, j * 8:(j + 1) * 8], h0s[:, j:j + 1])
            cp("pool", st_lru_s[:, c * 16:(c + 1) * 16].unsqueeze(2), v3(hh[:, 0:Wd], NSQ)[:, :, 7:8])
        p_ly = proj(1)
        act(gy[:, 0:Wd], p_ly, AF.Gelu_apprx_tanh)
        p_ga = proj(5)
        sigm(sga[:, 0:Wd], p_ga)
        tt("dve", oa[:, 0:Wd], hh[:, 0:Wd], gy[:, 0:Wd], ALU.mult)
        tt("pool", oa[:, 0:Wd], oa[:, 0:Wd], sga[:, 0:Wd], ALU.mult)

        for q in range(3):
            pq = proj(2 + q)
            if kind == "p":
                S3 = v3(shp[q][:, 0:1 + T], 1)
                if bi == 0:
                    mset("pool", shp[q][:, 0:1], 0.0)
                else:
                    Tp = blocks[bi - 1][1]
                    cp("pool", shp[q][:, 0:1], shp[q][:, Tp:Tp + 1])
            else:
                S3 = v3(shs[q][:, :], NSQ)
            cp("act", S3[:, :, 1:1 + T], v3(pq, ns))
            t3 = v3(tmp2[:, 0:Wd], ns)
            tt("pool", t3, S3[:, :, 0:T], S3[:, :, 1:1 + T], ALU.subtract)
            stt(v3(mx[q][:, 0:Wd], ns), t3, VV(l, V_MUR + q, c), S3[:, :, 1:1 + T], ALU.mult, ALU.add)
            if last_p:
                cp("pool", st_shift_p[:, q * 8 + c:q * 8 + c + 1], shp[q][:, T:T + 1])
            if kind == "s":
                o = (q * 8 + c) * 16
                cp("pool", st_shift_s[:, o:o + 16].unsqueeze(2), S3[:, :, 8:9])
        r_m, k_m, v_m = mx[0][:, 0:Wd], mx[1][:, 0:Wd], mx[2][:, 0:Wd]
        cs = slice(c * 128, (c + 1) * 128)
        p_d = bank()[:, 0:Wd]
        mm(p_d, smc[0:64, 0:128], loraA[0:64, t0:t0 + Wd])
        p_a = bank()[:, 0:Wd]
        mm(p_a, smc[64:128, 0:128], loraA[64:128, t0:t0 + Wd])
        p_g = bank()[:, 0:Wd]
        mm(p_g, smc[:, 128:256], loraB[:, t0:t0 + Wd])
        sigm(sg[:, 0:Wd], p_d, VV(l, V_NW0, c))
        sigm(asig[:, 0:Wd], p_a, VV(l, V_NA0, c))
        cp("act", gg[:, 0:Wd], p_g)
        ts("pool", kk[:, 0:Wd], k_m, VV(l, V_KK, c), ALU.mult)
        tt("pool", kk2[:, 0:Wd], kk[:, 0:Wd], kk[:, 0:Wd], ALU.mult)
        p_n = bank()[:, 0:Wd]
        mm(p_n, cf[:, C_BONE:C_BONE + 128], kk2[:, 0:Wd])
        ts("dve", rn[:, 0:Wd], p_n, 1e-24, ALU.max)
        act(rn[:, 0:Wd], rn[:, 0:Wd], AF.Ln)
        act(rn[:, 0:Wd], rn[:, 0:Wd], AF.Exp, scale=-0.5)
        tt("dve", kk[:, 0:Wd], kk[:, 0:Wd], rn[:, 0:Wd], ALU.mult)
        tt("dve", bb[:, 0:Wd], kk[:, 0:Wd], asig[:, 0:Wd], ALU.mult)
        ts("dve", kh[:, 0:Wd], asig[:, 0:Wd], VV(l, V_KA, c), ALU.mult, VV(l, V_OMKA, c), ALU.add)
        tt("dve", kh[:, 0:Wd], kh[:, 0:Wd], k_m, ALU.mult)
        rmask = cf[:, C_RMS:C_RMS + 128] if kind == "s" else None
        if kind == "p":
            for q in range(nch):
                scan(cum[:, q * 128:(q + 1) * 128], cf[:, C_RMP:C_RMP + 128], sg[:, q * 128:(q + 1) * 128], 0.0)
        else:
            scan(cum[:, 0:Wd], rmask, sg[:, 0:Wd], 0.0)
        act(gam[:, 0:Wd], cum[:, 0:Wd], AF.Exp, scale=-C0)
        act(ig[:, 0:Wd], cum[:, 0:Wd], AF.Exp, scale=C0)
        tt("pool", tmp[:, 0:Wd], cum[:, 0:Wd], sg[:, 0:Wd], ALU.subtract)
        act(e0[:, 0:Wd], tmp[:, 0:Wd], AF.Exp, scale=-C0)
        tt("dve", Rt[:, 0:Wd], r_m, gam[:, 0:Wd], ALU.mult)
        stt(At[:, 0:Wd], kk[:, 0:Wd], -1.0, e0[:, 0:Wd], ALU.mult, ALU.mult)
        tt("pool", Kt[:, 0:Wd], kh[:, 0:Wd], ig[:, 0:Wd], ALU.mult)
        tt("pool", Bt[:, 0:Wd], bb[:, 0:Wd], ig[:, 0:Wd], ALU.mult)
        stt(rk[:, 0:Wd], r_m, VV(l, V_RK, c), kh[:, 0:Wd], ALU.mult, ALU.mult)
        p_bon = bank()[:, 0:Wd]
        mm(p_bon, cf[:, C_BONE:C_BONE + 128], rk[:, 0:Wd])
        tt("dve", rk[:, 0:Wd], p_bon, v_m, ALU.mult)
        cp("pool", vbf[:, 0:Wd], v_m)

        role[0] = "b"
        S_f = Sp if kind == "p" else Ss
        S_b = Spb if kind == "p" else Ssb
        msu, mui, msl = (C_SU, C_UI, C_SL) if kind == "p" else (C_SSU, C_SUI, C_SSL)
        for q in range(nch):
            ck = slice(q * 128, (q + 1) * 128)
            first = (kind == "p" and bi == 0 and q == 0)
            if kind == "p":
                gend = gendc[:, bi % 2:bi % 2 + 1]
                cp("pool", gend, gam[:, q * 128 + 127:q * 128 + 128])
                ts("dve", Bh[:, :], Bt[:, ck], gend, ALU.mult)
                ts("dve", Kh[:, :], Kt[:, ck], gend, ALU.mult)
            else:
                cp("pool", gends[:, :].unsqueeze(2), v3(gam[:, 0:128], NSQ)[:, :, 7:8])
                g3 = gends[:, :].unsqueeze(2)
                tt("dve", v3(Bh[:, :], NSQ), v3(Bt[:, ck], NSQ), g3.to_broadcast([128, NSQ, 8]), ALU.mult)
                tt("dve", v3(Kh[:, :], NSQ), v3(Kt[:, ck], NSQ), g3.to_broadcast([128, NSQ, 8]), ALU.mult)
            tr(pst[:, 0:128], vbf[:, ck])
            tr(pst[:, 128:256], Bh[:, :])
            tr(pst[:, 256:384], Kh[:, :])
            cp("act", TT[:, 0:384], pst[:, 0:384])
            vT, BhT, KhT = TT[:, 0:128], TT[:, 128:256], TT[:, 256:384]
            bX = bank(); bY = bank(); bZ = bank()
            for hp in range(2):
                hs = slice(hp * 64, (hp + 1) * 64)
                mm(bX[:, hp * 128:(hp + 1) * 128], Bt[hs, ck], At[hs, ck])
                mm(bX[:, (2 + hp) * 128:(3 + hp) * 128], Kt[hs, ck], At[hs, ck])
                mm(bY[:, hp * 128:(hp + 1) * 128], Bt[hs, ck], Rt[hs, ck])
                mm(bY[:, (2 + hp) * 128:(3 + hp) * 128], Kt[hs, ck], Rt[hs, ck])
                mm(bZ[:, hp * 128:(hp + 1) * 128], At[hs, ck], Bt[hs, ck])
            m4 = lambda o: cb[:, o:o + 128].unsqueeze(1).to_broadcast([128, 4, 128])
            m2 = lambda o: cb[:, o:o + 128].unsqueeze(1).to_broadcast([128, 2, 128])
            r4 = lambda a: a.rearrange("p (a b) -> p a b", b=128)
            tt("dve", r4(MX[:, :]), r4(bX[:, 0:512]), m4(msu), ALU.mult)
            tt("dve", r4(MY[:, :]), r4(bY[:, 0:512]), m4(mui), ALU.mult)
            tt("dve", r4(MZ[:, :]), r4(bZ[:, 0:256]), m2(msl), ALU.mult)
            nlev = 6 if kind == "p" else 2
            cur = 0
            cp("pool", PP[0][:, 0:256], MX[:, 0:256])
            cp("pool", PP[0][:, 256:512], MZ[:, 0:256])
            tt("pool", r4(NN[0][:, :]), r4(MX[:, 0:256]), m2(C_ID), ALU.add)
            ncur = 0
            for i in range(nlev):
                Pc = PP[cur]
                bQ = bank()
                for hp in range(2):
                    hcs = slice(hp * 128, (hp + 1) * 128)
                    hts = slice(256 + hp * 128, 256 + (hp + 1) * 128)
                    if i < nlev - 1:
                        mm(bQ[:, hcs], Pc[:, hts], Pc[:, hcs])
                    mm(bQ[:, hts], Pc[:, hcs], Pc[:, hts])
                if i >= 1:
                    bR = bank()
                    for hp in range(2):
                        hcs = slice(hp * 128, (hp + 1) * 128)
                        hts = slice(256 + hp * 128, 256 + (hp + 1) * 128)
                        mm(bR[:, hcs], Pc[:, hts], NN[ncur][:, hcs])
                    tt("dve", NN[1 - ncur][:, :], bR[:, 0:256], NN[ncur][:, :], ALU.add)
                    ncur = 1 - ncur
                nxt = 1 - cur
                if i < nlev - 1:
                    cp("act", PP[nxt][:, :], bQ[:, 0:512])
                else:
                    cp("act", PP[nxt][:, 256:512], bQ[:, 256:512])
                cur = nxt
            bR = bank()
            for hp in range(2):
                hcs = slice(hp * 128, (hp + 1) * 128)
                hts = slice(256 + hp * 128, 256 + (hp + 1) * 128)
                mm(bR[:, hcs], PP[cur][:, hts], NN[ncur][:, hcs])
            tt("dve", NN[1 - ncur][:, :], bR[:, 0:256], NN[ncur][:, :], ALU.add)
            ncur = 1 - ncur
            Nf = NN[ncur]
            if kind == "s":
                b3 = lambda a: a.unsqueeze(1).to_broadcast([128, 8, 128])
                j3 = lambda a: a.rearrange("p (j s) -> p j s", j=8)

                def smf(half):
                    return cb[:, C_SMF + half * 1024:C_SMF + (half + 1) * 1024].rearrange("p (j s) -> p j s", j=8)

                def smt(half):
                    return cb[:, C_SMT + half * 8:C_SMT + (half + 1) * 8].unsqueeze(2).to_broadcast([128, 8, 128])
            bXT = bank()
            bXT2 = [bXT, bank()] if kind == "s" else None
            for hp in range(2):
                hs = slice(hp * 64, (hp + 1) * 64)
                o = bXT[:, hp * 64:(hp + 1) * 64] if kind == "p" else bXT2[hp][:, 0:64]
                if kind == "p":
                    if not first:
                        mm(o, At[hs, ck], S_b[hs, 0:64], start=True, stop=False)
                mm(o, MX[:, (2 + hp) * 128:(3 + hp) * 128], vT[:, hs], start=(first or kind == "s"), stop=(kind == "p"))
            if kind == "s":
                for half in range(2):
                    tt("pool", j3(Atj[:, :]), b3(At[:, ck]), smf(half), ALU.mult)
                    for hp in range(2):
                        hs = slice(hp * 64, (hp + 1) * 64)
                        o = bXT2[hp][:, 0:64]
                        for jj in range(8):
                            j = half * 8 + jj
                            mm(o, Atj[hs, jj * 128:(jj + 1) * 128], S_b[hs, j * 64:(j + 1) * 64], start=False, stop=(j == NSQ - 1))
            if kind == "p":
                cp("act", XTs[:, :], bXT[:, 0:128])
            else:
                cp("act", XTs[:, 0:64], bXT2[0][:, 0:64])
                cp("act", XTs[:, 64:128], bXT2[1][:, 0:64])
            bUT = bank()
            for hp in range(2):
                mm(bUT[:, hp * 64:(hp + 1) * 64], Nf[:, hp * 128:(hp + 1) * 128], XTs[:, hp * 64:(hp + 1) * 64])
            cp("act", UTs[:, :], bUT[:, 0:128])
            bO = bank()
            for hp in range(2):
                hs = slice(hp * 64, (hp + 1) * 64)
                o = bO[hs, 0:128]
                if kind == "p":
                    if not first:
                        mm(o, S_b[hs, 0:64], Rt[hs, ck], start=True, stop=False)
                mm(o, UTs[:, hs], MY[:, hp * 128:(hp + 1) * 128], start=(first or kind == "s"), stop=False)
                mm(o, vT[:, hs], MY[:, (2 + hp) * 128:(3 + hp) * 128], start=False, stop=(kind == "p"))
            if kind == "s":
                for half in range(2):
                    tt("pool", j3(Rtj[:, :]), b3(Rt[:, ck]), smf(half), ALU.mult)
                    for hp in range(2):
                        hs = slice(hp * 64, (hp + 1) * 64)
                        o = bO[hs, 0:128]
                        for jj in range(8):
                            j = half * 8 + jj
                            mm(o, S_b[hs, j * 64:(j + 1) * 64], Rtj[hs, jj * 128:(jj + 1) * 128], start=False, stop=(j == NSQ - 1))
            cp("act", of_[:, ck], bO[:, 0:128])
            if kind == "p":
                bS = bank()
                for hp in range(2):
                    hs = slice(hp * 64, (hp + 1) * 64)
                    mm(bS[hs, 0:64], BhT[:, hs], UTs[:, hs], start=True, stop=False)
                    mm(bS[hs, 0:64], KhT[:, hs], vT[:, hs], start=False, stop=True)
                if first:
                    cp("dve", S_f[:, :], bS[:, 0:64])
                else:
                    stt(S_f[:, :], S_f[:, :], gend, bS[:, 0:64], ALU.mult, ALU.add)
                cp("pool", S_b[:, :], S_f[:, :])
            else:
                gb = gends[:, :].unsqueeze(2).to_broadcast([128, NSQ, 64])
                tt("dve", v3(S_f[:, :], NSQ), v3(S_f[:, :], NSQ), gb, ALU.mult)
                for half in range(2):
                    tt("pool", j3(BhTj[:, :]), b3(BhT), smt(half), ALU.mult)
                    tt("pool", j3(KhTj[:, :]), b3(KhT), smt(half), ALU.mult)
                    bS = bank()
                    for hp in range(2):
                        hs = slice(hp * 64, (hp + 1) * 64)
                        for jj in range(8):
                            mm(bS[hs, jj * 64:(jj + 1) * 64], BhTj[:, jj * 128 + hp * 64:jj * 128 + (hp + 1) * 64], UTs[:, hs], start=True, stop=False)
                            mm(bS[hs, jj * 64:(jj + 1) * 64], KhTj[:, jj * 128 + hp * 64:jj * 128 + (hp + 1) * 64], vT[:, hs], start=False, stop=True)
                    tt("dve", S_f[:, half * 512:(half + 1) * 512], S_f[:, half * 512:(half + 1) * 512], bS[:, 0:512], ALU.add)
        if last_p:
            P.dma("sp", o_wkv_p[l, c], Sp[:, :], is_output=True)
        if kind == "s":
            P.dma("sp", o_wkv_s[l, c], Ss[:, :], is_output=True)

        tt("pool", o2[:, 0:Wd], of_[:, 0:Wd], of_[:, 0:Wd], ALU.mult)
        p_mu = bank()[:, 0:Wd]
        mm(p_mu, cf[:, C_BONE64:C_BONE64 + 128], of_[:, 0:Wd])
        p_m2 = bank()[:, 0:Wd]
        mm(p_m2, cf[:, C_BONE64:C_BONE64 + 128], o2[:, 0:Wd])
        cp("act", mus[:, 0:Wd], p_mu)
        tt("pool", var[:, 0:Wd], mus[:, 0:Wd], mus[:, 0:Wd], ALU.mult)
        tt("dve", var[:, 0:Wd], p_m2, var[:, 0:Wd], ALU.subtract)
        act(var[:, 0:Wd], var[:, 0:Wd], AF.Ln, bias=cf_eps_gn[:, 0:1])
        act(var[:, 0:Wd], var[:, 0:Wd], AF.Exp, scale=-0.5)
        tt("dve", of_[:, 0:Wd], of_[:, 0:Wd], mus[:, 0:Wd], ALU.subtract)
        tt("dve", of_[:, 0:Wd], of_[:, 0:Wd], var[:, 0:Wd], ALU.mult)
        ts("dve", of_[:, 0:Wd], of_[:, 0:Wd], VV(l, V_GW, c), ALU.mult, VV(l, V_GB, c), ALU.add)
        tt("pool", of_[:, 0:Wd], of_[:, 0:Wd], rk[:, 0:Wd], ALU.add)
        tt("pool", of_[:, 0:Wd], of_[:, 0:Wd], gg[:, 0:Wd], ALU.mult)
        p_gb = proj(6)
        sigm(sgb[:, 0:Wd], p_gb)
        tt("dve", of_[:, 0:Wd], of_[:, 0:Wd], sgb[:, 0:Wd], ALU.mult)
        tt("dve", mrg[:, 0:Wd], of_[:, 0:Wd], oa[:, 0:Wd], ALU.add)
        wo_c = wob[c % 2]
        for oc in range(8):
            b = bank()[:, 0:Wd]
            mm(b, wo_c[:, oc * 128:(oc + 1) * 128], mrg[:, 0:Wd])
            xs = xf[:, oc, t0:t0 + Wd]
            if c == 0:
                stt(xs, xs, ALPHA, b, ALU.mult, ALU.add)
            else:
                tt("dve", xs, b, xs, ALU.add)
        role[0] = "all"

    cf_eps_gn = sb("epsgn", 1)
    cf_eps_ln = sb("epsln", 1)
    mset("pool", cf_eps_gn[:, :], GN_EPS)
    mset("pool", cf_eps_ln[:, :], LN_EPS)

    def layer_norm(l, vg, vb):
        for t0 in range(0, NTOK, 256):
            Wd = min(256, NTOK - t0)
            p_s = bank()[:, 0:Wd]
            for oc in range(8):
                mm(p_s, cf[:, C_ONESD:C_ONESD + 128], xf[:, oc, t0:t0 + Wd], start=(oc == 0), stop=(oc == 7))
            p_q = bank()[:, 0:Wd]
            for oc in range(8):
                tt("pool", lnq[:, 0:Wd], xf[:, oc, t0:t0 + Wd], xf[:, oc, t0:t0 + Wd], ALU.mult)
                mm(p_q, cf[:, C_ONESD:C_ONESD + 128], lnq[:, 0:Wd], start=(oc == 0), stop=(oc == 7))
            cp("act", lnm[:, 0:Wd], p_s)
            tt("pool", lnr[:, 0:Wd], lnm[:, 0:Wd], lnm[:, 0:Wd], ALU.mult)
            tt("dve", lnr[:, 0:Wd], p_q, lnr[:, 0:Wd], ALU.subtract)
            act(lnr[:, 0:Wd], lnr[:, 0:Wd], AF.Sqrt, bias=cf_eps_ln[:, 0:1])
            recip(lnr[:, 0:Wd], lnr[:, 0:Wd])
            for oc in range(8):
                xs = xf[:, oc, t0:t0 + Wd]
                tt("dve", xs, xs, lnm[:, 0:Wd], ALU.subtract)
                tt("dve", xs, xs, lnr[:, 0:Wd], ALU.mult)
                ts("dve", xs, xs, VV(l, vg, oc), ALU.mult, VV(l, vb, oc), ALU.add)
                cp("pool", xb[:, oc, t0:t0 + Wd], xs)

    for l in range(NL_RUN):
        P.dma("pool", wlb, wl[l])
        P.dma("pool", wcb[0], wc[l, 0])
        P.dma("pool", wob[0][:], wo[l, 0])
        P.dma("pool", smw[0][:], smallw[l, 0])
        P.dma("sp", sshl[:, :], sshift[l])
        P.dma("sp", slrl[:, :], slru[l])
        for q in range(2):
            cp("pool", v3(shls[q][:, :], NSQ)[:, :, 0:1], sshl[:, (24 + q) * 16:(25 + q) * 16].unsqueeze(2))
        for bi, (t0, Wd, kind) in enumerate(blocks):
            ns = 1 if kind == "p" else NSQ
            T = Wd // ns
            for q in range(2):
                b = bank()[:, 0:Wd]
                for kc in range(8):
                    mm(b, wlb[:, kc * 256 + q * 128:kc * 256 + (q + 1) * 128], xb[:, kc, t0:t0 + Wd], start=(kc == 0), stop=(kc == 7))
                if kind == "p":
                    S3 = v3(shlp[q][:, 0:1 + T], 1)
                    if bi == 0:
                        mset("pool", shlp[q][:, 0:1], 0.0)
                    else:
                        Tp = blocks[bi - 1][1]
                        cp("pool", shlp[q][:, 0:1], shlp[q][:, Tp:Tp + 1])
                else:
                    S3 = v3(shls[q][:, :], NSQ)
                cp("act", S3[:, :, 1:1 + T], v3(b, ns))
                t3 = v3(tmp[:, 0:Wd], ns)
                tt("pool", t3, S3[:, :, 0:T], S3[:, :, 1:1 + T], ALU.subtract)
                stt(v3(u[:, 0:Wd], ns), t3, VV(l, V_MUL, q), S3[:, :, 1:1 + T], ALU.mult, ALU.add)
                if q == 0:
                    act(loraA[0:64, t0:t0 + Wd], u[0:64, 0:Wd], AF.Tanh)
                    cp("act", loraA[64:128, t0:t0 + Wd], u[64:128, 0:Wd])
                else:
                    act(loraB[:, t0:t0 + Wd], u[:, 0:Wd], AF.Sigmoid)
                if kind == "p" and bi == nblk - 2:
                    cp("pool", st_shift_p[:, 24 + q:25 + q], shlp[q][:, T:T + 1])
                if kind == "s":
                    cp("pool", st_shift_s[:, (24 + q) * 16:(25 + q) * 16].unsqueeze(2), S3[:, :, 8:9])
        for c in range(DBG_NC):
            if c + 1 < DBG_NC:
                P.dma("pool", wcb[(c + 1) % 2], wc[l, c + 1])
                P.dma("pool", wob[(c + 1) % 2][:], wo[l, c + 1])
                P.dma("pool", smw[(c + 1) % 2][:], smallw[l, c + 1])
            P.dma("sp", scv[:, :], sconv[l, c])
            cp("pool", v3(lxs[:, :], NSQ)[:, :, 0:3], v3(scv[:, :], NSQ))
            cp("pool", h0s[:, :], slrl[:, c * 16:(c + 1) * 16])
            for q in range(3):
                o = (q * 8 + c) * 16
                cp("pool", v3(shs[q][:, :], NSQ)[:, :, 0:1], sshl[:, o:o + 16].unsqueeze(2))
            P.dma("sp", Ss[:, :], swkv[l, c])
            cp("pool", Ssb[:, :], Ss[:, :])
            for bi in (range(nblk) if DBG_NB is None else DBG_NB):
                mixer_block(l, c, bi)
        P.dma("sp", o_lru_p[l], st_lru_p[:, :], is_output=True)
        P.dma("sp", o_lru_s[l], st_lru_s[:, :], is_output=True)
        P.dma("sp", o_shift_p[l], st_shift_p[:, :], is_output=True)
        P.dma("sp", o_shift_s[l], st_shift_s[:, :], is_output=True)
        if not DBG_POST:
            continue
        layer_norm(l, V_L1G, V_L1B)
        for e8 in range(8):
            P.dma("pool", w1b[e8 % 2], w1[l, e8])
            P.dma("pool", w2b[e8 % 2], w2[l, e8])
            for t0 in range(0, NTOK, 256):
                Wd = min(256, NTOK - t0)
                for jj in range(4):
                    b = bank()[:, 0:Wd]
                    for kc in range(8):
                        mm(b, w1b[e8 % 2][:, kc * 512 + jj * 128:kc * 512 + (jj + 1) * 128], xb[:, kc, t0:t0 + Wd], start=(kc == 0), stop=(kc == 7))
                    act(hr[:, 0:Wd], b, AF.Relu)
                    tt("pool", hid[jj][:, 0:Wd], hr[:, 0:Wd], hr[:, 0:Wd], ALU.mult)
                for oc in range(8):
                    b = bank()[:, 0:Wd]
                    for jj in range(4):
                        mm(b, w2b[e8 % 2][:, jj * 1024 + oc * 128:jj * 1024 + (oc + 1) * 128], hid[jj][:, 0:Wd], start=(jj == 0), stop=(jj == 3))
                    xs = xf[:, oc, t0:t0 + Wd]
                    if e8 == 0:
                        stt(xs, xs, ALPHA, b, ALU.mult, ALU.add)
                    else:
                        tt("dve", xs, b, xs, ALU.add)
        layer_norm(l, V_L2G, V_L2B)
    for kc in range(8):
        P.dma("sp", yT[:, kc, :], xf[:, kc, :], is_output=True)
    P.emit()
    st.close()
    print("ops:", len(P.all), "sim_time_us:", getattr(P, "sim_time", None))
    return nc


def _prep_shared(inp):
    f = lambda k: np.asarray(inp[k], np.float32)
    w_in = f("w_in")
    V = np.zeros((L, NV, 1024), np.float32)
    V[:, 0:4] = f("conv_w")
    V[:, V_CB] = f("conv_b")
    V[:, V_BA] = f("lru_ba").reshape(L, 1024)
    V[:, V_BX] = f("lru_bx").reshape(L, 1024)
    V[:, V_AP] = f("lru_a_param")
    V[:, V_W0] = f("w0"); V[:, V_A0] = f("a0"); V[:, V_KK] = f("k_k"); V[:, V_KA] = f("k_a")
    V[:, V_RK] = f("r_k").reshape(L, 1024)
    V[:, V_GW] = f("gn_w"); V[:, V_GB] = f("gn_b")
    V[:, V_L1G] = f("ln1_g"); V[:, V_L1B] = f("ln1_b"); V[:, V_L2G] = f("ln2_g"); V[:, V_L2B] = f("ln2_b")
    mu = f("shift_mu")
    V[:, V_MUR] = mu[:, 0:1024]; V[:, V_MUR + 1] = mu[:, 1024:2048]; V[:, V_MUR + 2] = mu[:, 2048:3072]
    V[:, V_MUL, 0:256] = mu[:, 3072:3328]
    vecs = np.ascontiguousarray(V.reshape(L, NV, 8, 128).transpose(3, 0, 1, 2).reshape(128, L * NV * 8))
    offs = [0, 1024, 2048, 3072, 4096, 5376, 6400]
    wc = np.empty((L, 8, 128, 8 * 896), np.float32)
    for c in range(8):
        cols = np.concatenate([np.arange(o + c * 128, o + (c + 1) * 128) for o in offs])
        sel = w_in[:, :, cols]
        wc[:, c] = sel.reshape(L, 8, 128, 896).transpose(0, 2, 1, 3).reshape(L, 128, 8 * 896)
    wl = np.ascontiguousarray(w_in[:, :, 5120:5376].reshape(L, 8, 128, 256).transpose(0, 2, 1, 3).reshape(L, 128, 2048))
    wo = np.ascontiguousarray(f("w_out").reshape(L, 8, 128, 1024))
    upab = np.ascontiguousarray(np.concatenate([f("decay_up"), f("aaa_up")], axis=1))
    upg = np.ascontiguousarray(f("gate_up"))
    lruw = np.zeros((L, 128, 2, 8, 128), np.float32)
    for g, key in enumerate(("lru_wa", "lru_wx")):
        wg = f(key)
        for c in range(8):
            for hp in range(2):
                lruw[:, hp * 64:(hp + 1) * 64, g, c, hp * 64:(hp + 1) * 64] = wg[:, 2 * c + hp]
    smallw = np.empty((L, 8, 128, 512), np.float32)
    for c in range(8):
        smallw[:, c, :, 0:128] = upab[:, :, c * 128:(c + 1) * 128]
        smallw[:, c, :, 128:256] = upg[:, :, c * 128:(c + 1) * 128]
        smallw[:, c, :, 256:384] = lruw[:, :, 0, c, :]
        smallw[:, c, :, 384:512] = lruw[:, :, 1, c, :]
    w1 = np.ascontiguousarray(f("mlp_w1").reshape(L, 8, 128, 8, 512).transpose(0, 3, 2, 1, 4).reshape(L, 8, 128, 4096))
    w2 = np.ascontiguousarray(f("mlp_w2").reshape(L, 8, 4, 128, 1024).transpose(0, 1, 3, 2, 4).reshape(L, 8, 128, 4096))
    cc, cd = make_consts()
    return dict(vecs=vecs, cst=cc, cstb=cd, wc=wc, wl=wl, wo=wo, smallw=smallw, w1=w1, w2=w2)


_NC_CACHE = {}


def kernel(**inp):
    f = lambda k: np.asarray(inp[k], np.float32)
    shared = _prep_shared(inp)
    xp, xs = f("x_prompt"), f("x_sample")
    s_conv, s_lru, s_shift, s_wkv = f("state_conv"), f("state_lru"), f("state_shift"), f("state_wkv")
    in_maps = []
    for core in range(8):
        cs = slice(core * 16, (core + 1) * 16)
        xtok = np.concatenate([xp[core], xs[cs].reshape(128, 1024)], axis=0)
        m = dict(shared)
        m["xT"] = np.ascontiguousarray(xtok.T.reshape(8, 128, NTOK).transpose(1, 0, 2))
        m["sconv"] = np.ascontiguousarray(s_conv[:, cs].reshape(L, 16, 3, 8, 128).transpose(0, 3, 4, 1, 2).reshape(L, 8, 128, 48))
        m["slru"] = np.ascontiguousarray(s_lru[:, cs].reshape(L, 16, 8, 128).transpose(0, 3, 2, 1).reshape(L, 128, 128))
        m["sshift"] = np.ascontiguousarray(s_shift[:, cs].reshape(L, 16, 26, 128).transpose(0, 3, 2, 1).reshape(L, 128, 416))
        m["swkv"] = np.ascontiguousarray(s_wkv[:, cs].reshape(L, 16, 8, 2, 64, 64).transpose(0, 2, 3, 5, 1, 4).reshape(L, 8, 128, 1024))
        in_maps.append(m)
    if "nc" not in _NC_CACHE:
        _NC_CACHE["nc"] = build_program()
    res = run_bass_kernel_spmd(_NC_CACHE["nc"], in_maps, core_ids=list(range(8)))
    R = res.results
    y_p = np.empty((8, NPR, 1024), np.float32); y_s = np.empty((128, 8, 1024), np.float32)
    conv_p = np.empty((L, 8, 3, 1024), np.float32); lru_p = np.empty((L, 8, 1024), np.float32)
    shift_p = np.empty((L, 8, 3328), np.float32); wkv_p = np.empty((L, 8, 16, 64, 64), np.float32)
    conv_s = np.empty((L, 128, 3, 1024), np.float32); lru_s = np.empty((L, 128, 1024), np.float32)
    shift_s = np.empty((L, 128, 3328), np.float32); wkv_s = np.empty((L, 128, 16, 64, 64), np.float32)
    for core in range(8):
        r = R[core]
        cs = slice(core * 16, (core + 1) * 16)
        ytok = np.asarray(r["yT"]).reshape(128, 8, NTOK).transpose(2, 1, 0).reshape(NTOK, 1024)
        y_p[core] = ytok[:NPR]
        y_s[cs] = ytok[NPR:].reshape(16, 8, 1024)
        conv_p[:, core] = np.asarray(r["o_conv_p"]).reshape(L, 8, 128, 3).transpose(0, 3, 1, 2).reshape(L, 3, 1024)
        lru_p[:, core] = np.asarray(r["o_lru_p"]).reshape(L, 128, 8).transpose(0, 2, 1).reshape(L, 1024)
        shift_p[:, core] = np.asarray(r["o_shift_p"]).reshape(L, 128, 26).transpose(0, 2, 1).reshape(L, 3328)
        wkv_p[:, core] = np.asarray(r["o_wkv_p"]).reshape(L, 8, 2, 64, 64).transpose(0, 1, 2, 4, 3).reshape(L, 16, 64, 64)
        conv_s[:, cs] = np.asarray(r["o_conv_s"]).reshape(L, 8, 128, 16, 3).transpose(0, 3, 4, 1, 2).reshape(L, 16, 3, 1024)
        lru_s[:, cs] = np.asarray(r["o_lru_s"]).reshape(L, 128, 8, 16).transpose(0, 3, 2, 1).reshape(L, 16, 1024)
        shift_s[:, cs] = np.asarray(r["o_shift_s"]).reshape(L, 128, 26, 16).transpose(0, 3, 2, 1).reshape(L, 16, 3328)
        wkv_s[:, cs] = np.asarray(r["o_wkv_s"]).reshape(L, 8, 2, 64, 16, 64).transpose(0, 4, 1, 2, 5, 3).reshape(L, 16, 16, 64, 64)
    return (y_p, y_s, conv_p, lru_p, shift_p, wkv_p, conv_s, lru_s, shift_s, wkv_s)
```

```python
import heapq
import numpy as np
import concourse.bass as bass
import concourse.mybir as mybir

F32 = mybir.dt.float32
BF16 = mybir.dt.bfloat16
AF = mybir.ActivationFunctionType
ALU = mybir.AluOpType

ENGS = ("pe", "act", "dve", "pool", "sp")
N_DMA_SEMS = 12
EPOCH = 30000
SCHEDULE = True
SEM_LAT = 0.35
DEBUG_LINES = False
DBG_FREE = [0, 0]
CHAIN = {}


def region_of(ap):
    pat = ap.ap
    pitch, npart = pat[0]
    off = int(ap.offset)
    if pitch == 0:
        pitch = 1 << 40
    p_lo = off // pitch
    f_lo = off % pitch
    span = 0
    for step, cnt in pat[1:]:
        span += (cnt - 1) * abs(step)
    return (ap.name, p_lo, p_lo + npart, f_lo, f_lo + span + 1)


def _overlap(a, b):
    return a[1] < b[2] and b[1] < a[2] and a[3] < b[4] and b[3] < a[4]


def _covers(a, b):
    return a[1] <= b[1] and a[2] >= b[2] and a[3] <= b[3] and a[4] >= b[4]


def _onchip(ap):
    return ap is not None and str(getattr(ap, "space", "")) in ("SB", "PSUM")


class Op:
    __slots__ = ("id", "eng", "fn", "preds", "nsucc", "succs", "is_dma", "occ", "lat", "prio", "line",
                 "start", "fin", "sem", "val", "waits", "npend", "is_out", "mode")


class Prog:
    def __init__(self, nc):
        self.nc = nc
        self.all = []
        self.live = {}
        self.liver = {}
        self.pe_bank = {}

    def _track(self, o, reads, writes):
        preds = set()
        for ap in reads:
            if not _onchip(ap):
                continue
            r = region_of(ap)
            for ent in self.live.get(r[0], ()):
                if ent[1] == "w" and _overlap(ent[0], r):
                    preds.add(ent[2])
        for ap in writes:
            if not _onchip(ap):
                continue
            r = region_of(ap)
            for ent in self.live.get(r[0], ()):
                if _overlap(ent[0], r):
                    preds.add(ent[2])
            for ent in self.liver.get(r[0], ()):
                if _overlap(ent[0], r):
                    preds.add(ent[2])
            if o.eng == "pe":
                key = (r[0], r[3] // (512 if ap.dtype == F32 else 1024))
                prev = self.pe_bank.get(key)
                if prev is not None:
                    preds.add(prev)
                self.pe_bank[key] = o
        ch = CHAIN.get(o.eng)
        if ch is not None:
            if ch:
                if not (o.eng == "dve" and DBG_FREE[0] <= o.id < DBG_FREE[1]):
                    preds.add(ch[0])
                ch[0] = o
            else:
                ch.append(o)
        preds.discard(o)
        o.preds = preds
        for ap in reads:
            if not _onchip(ap):
                continue
            r = region_of(ap)
            self.liver.setdefault(r[0], []).append([r, "r", o])
        for ap in writes:
            if not _onchip(ap):
                continue
            r = region_of(ap)
            lst = self.live.setdefault(r[0], [])
            lst[:] = [ent for ent in lst if not _covers(r, ent[0])]
            lst.append([r, "w", o])
            lr = self.liver.get(r[0])
            if lr:
                lr[:] = [ent for ent in lr if not _covers(r, ent[0])]

    def op(self, eng, fn, reads=(), writes=(), cost=0.3, mode=0):
        o = Op()
        o.id = len(self.all)
        o.eng, o.fn, o.is_dma, o.is_out = eng, fn, False, False
        o.mode = mode
        o.occ = cost
        o.lat = cost + (0.1 if eng == "pe" else 0.25)
        if DEBUG_LINES:
            import sys as _s
            o.line = _s._getframe(2).f_lineno
        self._track(o, reads, writes)
        self.all.append(o)
        return o

    def dma(self, eng, out, in_, is_output=False, nbytes=0):
        o = Op()
        o.id = len(self.all)
        o.eng, o.is_dma, o.is_out = eng, True, is_output
        o.mode = 0
        o.fn = lambda e, out=out, in_=in_: e.dma_start(out=out, in_=in_)
        if not nbytes:
            nbytes = 4
            for d in out.shape:
                nbytes *= d
        if DEBUG_LINES:
            import sys as _s
            o.line = _s._getframe(1).f_lineno
        o.occ = 1.5 if eng == "pool" else 0.15
        o.lat = o.occ + 2.0 + nbytes / 100e3
        self._track(o, [in_], [out])
        self.all.append(o)
        return o

    def schedule(self):
        ops = self.all
        for o in ops:
            o.succs = []
        for o in ops:
            for p in o.preds:
                p.succs.append(o)
        for o in reversed(ops):
            m = 0.0
            for s in o.succs:
                if s.prio > m:
                    m = s.prio
            o.prio = m + o.lat
        order = {e: [] for e in ENGS}
        if not SCHEDULE:
            for o in ops:
                order[o.eng].append(o)
            return order
        cursor = {e: 0.0 for e in ENGS}
        avail = {e: [] for e in ENGS}
        ready_t = {}
        for o in ops:
            o.npend = len(o.preds)
            if o.npend == 0:
                ready_t[o.id] = 0.0
                heapq.heappush(avail[o.eng], (-o.prio, o.id, o))
        nleft = len(ops)
        while nleft:
            best_e = None
            for e in ENGS:
                if avail[e] and (best_e is None or cursor[e] < cursor[best_e]):
                    best_e = e
            e = best_e
            t = cursor[e]
            h = avail[e]
            pick = None
            popped = []
            earliest = None
            for _ in range(min(len(h), 24)):
                item = heapq.heappop(h)
                popped.append(item)
                rt = ready_t[item[1]]
                if rt <= t + 1e-9:
                    pick = item
                    break
                if earliest is None or rt < ready_t[earliest[1]]:
                    earliest = item
            if pick is None:
                pick = earliest
            for item in popped:
                if item is not pick:
                    heapq.heappush(h, item)
            o = pick[2]
            st = max(t, ready_t[o.id])
            o.start = st
            o.fin = st + o.lat
            cursor[e] = st + o.occ
            order[e].append(o)
            nleft -= 1
            for s in o.succs:
                s.npend -= 1
                if s.npend == 0:
                    rt = 0.0
                    for p in s.preds:
                        f = p.fin + (SEM_LAT if p.eng != s.eng else 0.0)
                        if f > rt:
                            rt = f
                    ready_t[s.id] = rt
                    heapq.heappush(avail[s.eng], (-s.prio, s.id, s))
        self.sim_time = max(o.fin for o in ops)
        return order

    def emit(self):
        nc = self.nc
        from contextlib import ExitStack

        order = self.schedule()
        self.order = order
        nep = {}
        for e in ENGS:
            cnt, ep = 0, 0
            uses = [0] * N_DMA_SEMS
            rr = 0
            for o in order[e]:
                if o.is_dma:
                    k = rr
                    rr = (rr + 1) % N_DMA_SEMS
                    o.waits = {}
                    if uses[k] > 0:
                        o.waits[("D", e, k)] = 16 * uses[k]
                    uses[k] += 1
                    o.sem, o.val = ("D", e, k), 16 * uses[k]
                else:
                    cnt += 1
                    if cnt > EPOCH:
                        cnt, ep = 1, ep + 1
                    o.sem, o.val = ("E", e, ep), cnt
                    o.waits = {}
            nep[e] = ep + 1
        for e in ENGS:
            waited = {}
            prev = None
            for o in order[e]:
                w = o.waits
                if e == "pe":
                    if prev is not None and (prev.mode != o.mode or o.mode == 2):
                        w[prev.sem] = max(w.get(prev.sem, 0), prev.val)
                    prev = o
                for p in o.preds:
                    if p.eng == "pe" and e == "pe":
                        continue
                    if w.get(p.sem, 0) < p.val:
                        w[p.sem] = p.val
                o.waits = []
                for k, v in sorted(w.items()):
                    if waited.get(k, 0) < v:
                        waited[k] = v
                        o.waits.append((k, v))

        with ExitStack() as st:
            sems = {}
            for e in ENGS:
                for ep in range(nep[e]):
                    sems[("E", e, ep)] = st.enter_context(nc.semaphore("sem_%s_%d" % (e, ep)))
            for e in ("sp", "pool"):
                for k in range(N_DMA_SEMS):
                    sems[("D", e, k)] = st.enter_context(nc.semaphore("semd_%s_%d" % (e, k)))
            block = st.enter_context(nc.Block())

            def run(engname, e):
                for o in order[engname]:
                    for k, v in o.waits:
                        e.wait_ge(sems[k], v)
                    ins = o.fn(e)
                    ins.then_inc(sems[o.sem], 16 if o.is_dma else 1)
                if engname == "sp":
                    final = {}
                    for o in self.all:
                        if o.is_out:
                            final[o.sem] = max(final.get(o.sem, 0), o.val)
                    for k, v in final.items():
                        e.wait_ge(sems[k], v)

            @block.tensor
            def _(e):
                run("pe", e)

            @block.scalar
            def _(e):
                run("act", e)

            @block.vector
            def _(e):
                run("dve", e)

            @block.gpsimd
            def _(e):
                run("pool", e)

            @block.sync
            def _(e):
                run("sp", e)

from concourse.bass_utils import run_bass_kernel_spmd

from contextlib import ExitStack

L = 4
NL_RUN = 4
DBG_NC = 8
DBG_NB = None
DBG_POST = True
D = 1024
NTOK = 2176
NPR = 2048
NSQ = 16
TS = 8
WB = 128
NV = 29
ALPHA = (2 * L) ** 0.25
C0 = 0.6065306597126334
LN_EPS = 1e-5
GN_EPS = 64e-5
V_CW, V_CB, V_BA, V_BX, V_AP, V_W0, V_A0, V_KK, V_KA, V_RK, V_GW, V_GB = 0, 4, 5, 6, 7, 8, 9, 10, 11, 12, 13, 14
V_L1G, V_L1B, V_L2G, V_L2B, V_MUR, V_MUL, V_CP, V_OMKA = 15, 16, 17, 18, 19, 22, 23, 24
V_NBA, V_NBX, V_NW0, V_NA0 = 25, 26, 27, 28

C_BONE, C_BONE64, C_ONESD, C_RMP, C_RMS = [i * 128 for i in range(5)]
NCF = 5 * 128
C_ID, C_SU, C_UI, C_SL, C_SSU, C_SUI, C_SSL = [i * 128 for i in range(7)]
C_SMF = 7 * 128
C_SMT = C_SMF + 16 * 128
NCB = C_SMT + 16


def make_consts():
    c = np.zeros((128, NCF), np.float32)
    d = np.zeros((128, NCB), np.float32)
    i = np.arange(128)
    s, t = i[:, None], i[None, :]
    d[:, C_ID:C_ID + 128] = (s == t)
    d[:, C_SU:C_SU + 128] = (s < t)
    d[:, C_UI:C_UI + 128] = (s <= t)
    d[:, C_SL:C_SL + 128] = (s > t)
    same = (s // 8) == (t // 8)
    d[:, C_SSU:C_SSU + 128] = (s < t) & same
    d[:, C_SUI:C_SUI + 128] = (s <= t) & same
    d[:, C_SSL:C_SSL + 128] = (s > t) & same
    c[:, C_BONE:C_BONE + 128] = ((s // 64) == (t // 64))
    c[:, C_BONE64:C_BONE64 + 128] = ((s // 64) == (t // 64)) / 64.0
    c[:, C_ONESD:C_ONESD + 128] = 1.0 / D
    c[:, C_RMP:C_RMP + 128] = (t % 128 != 0)
    c[:, C_RMS:C_RMS + 128] = (t % 8 != 0)
    for j in range(16):
        d[:, C_SMF + j * 128:C_SMF + (j + 1) * 128] = ((t // 8) == j)
        d[:, C_SMT + j] = ((i // 8) == j)
    return c, d


def build_program():
    nc = bass.Bass("TRN2", target_bir_lowering=False)

    def din(name, shape):
        return nc.dram_tensor(name, shape, F32, kind="ExternalInput").ap()

    def dout(name, shape):
        return nc.dram_tensor(name, shape, F32, kind="ExternalOutput").ap()

    xT = din("xT", [128, 8, NTOK])
    vecs = din("vecs", [128, L * NV * 8])
    cst = din("cst", [128, NCF])
    cstb = din("cstb", [128, NCB])
    wc = din("wc", [L, 8, 128, 8 * 896])
    wl = din("wl", [L, 128, 8 * 256])
    wo = din("wo", [L, 8, 128, 1024])
    smallw = din("smallw", [L, 8, 128, 512])
    w1 = din("w1", [L, 8, 128, 8 * 512])
    w2 = din("w2", [L, 8, 128, 4 * 1024])
    sconv = din("sconv", [L, 8, 128, 48])
    slru = din("slru", [L, 128, 128])
    sshift = din("sshift", [L, 128, 26 * 16])
    swkv = din("swkv", [L, 8, 128, 1024])
    yT = dout("yT", [128, 8, NTOK])
    o_conv_p = dout("o_conv_p", [L, 8, 128, 3])
    o_lru_p = dout("o_lru_p", [L, 128, 8])
    o_shift_p = dout("o_shift_p", [L, 128, 26])
    o_wkv_p = dout("o_wkv_p", [L, 8, 128, 64])
    o_conv_s = dout("o_conv_s", [L, 8, 128, 48])
    o_lru_s = dout("o_lru_s", [L, 128, 128])
    o_shift_s = dout("o_shift_s", [L, 128, 26 * 16])
    o_wkv_s = dout("o_wkv_s", [L, 8, 128, 1024])

    st = ExitStack()
    P = Prog(nc)
    tot = [0]

    def sb(name, cols, dt=F32):
        tot[0] += cols * (4 if dt == F32 else 2)
        return st.enter_context(nc.sbuf_tensor(name, [128, cols], dt))

    xf = st.enter_context(nc.sbuf_tensor("xf", [128, 8, NTOK], F32)); tot[0] += 8 * NTOK * 4
    xb = st.enter_context(nc.sbuf_tensor("xb", [128, 8, NTOK], BF16)); tot[0] += 8 * NTOK * 2
    vec = sb("vec", L * NV * 8)
    cf = sb("cf", NCF)
    cb = sb("cb", NCB, BF16)
    loraA = sb("loraA", NTOK, BF16)
    loraB = sb("loraB", NTOK, BF16)
    smw = [sb("smw%d" % i, 512, BF16) for i in range(2)]
    wob = [sb("wob%d" % i, 1024, BF16) for i in range(2)]
    AR = sb("AR", 16384, BF16)
    wcb = [AR[:, i * 7168:(i + 1) * 7168] for i in range(2)]
    wlb = AR[:, 14336:16384]
    w1b = [AR[:, i * 8192:i * 8192 + 4096] for i in range(2)]
    w2b = [AR[:, i * 8192 + 4096:(i + 1) * 8192] for i in range(2)]
    ps = st.enter_context(nc.psum_tensor("ps", [128, 7, 512], F32))
    pst = st.enter_context(nc.psum_tensor("pst", [128, 1024], BF16))
    psrr = [0]

    role = ["all"]
    rr_a = [0]
    rr_b = [0]

    def bank():
        if role[0] == "a":
            b = rr_a[0]
            rr_a[0] = (b + 1) % 3
        elif role[0] == "b":
            b = 3 + rr_b[0]
            rr_b[0] = (rr_b[0] + 1) % 4
        else:
            b = psrr[0]
            psrr[0] = (b + 1) % 7
        return ps[:, b, :]

    def nfree(ap):
        n = 1
        for d in ap.shape[1:]:
            n *= d
        return n

    def inps(ap):
        return str(ap.space) == "PSUM"

    def mm(out, lhsT, rhs, start=True, stop=True):
        c = max(64, nfree(rhs)) / 1400.0 * (4 if lhsT.dtype == F32 else 1) + 0.02
        P.op("pe", lambda e: e.matmul(out, lhsT, rhs, start=start, stop=stop), reads=[lhsT, rhs], writes=[out], cost=c,
             mode=(1 if lhsT.dtype == F32 else (0 if lhsT.shape[0] == 128 else 10 + region_of(lhsT)[1] // 32)))

    def tr(out, in_):
        ident = cb[0:in_.shape[0], C_ID:C_ID + in_.shape[0]]
        P.op("pe", lambda e: e.transpose(out, in_, ident), reads=[in_, ident], writes=[out], cost=0.15, mode=2)

    def act(out, in_, func, bias=None, scale=None):
        kw = {}
        rd = [in_]
        c = 0.22 + nfree(in_) / 1400.0
        if bias is not None:
            kw["bias"] = bias
            if not isinstance(bias, (int, float)):
                rd.append(bias)
                c += 0.09
        if scale is not None:
            kw["scale"] = scale
            if not isinstance(scale, (int, float)):
                rd.append(scale)
                c += 0.09
        P.op("act", lambda e: e.activation(out=out, in_=in_, func=func, **kw), reads=rd, writes=[out], cost=c)

    def ecost(eng, n, two=False):
        if eng == "pool":
            return 0.15 + n / 350.0
        return 0.09 + n * (2 if two else 1) / 960.0

    def tt(eng, out, in0, in1, op):
        two = not (inps(in0) or inps(in1))
        P.op(eng, lambda e: e.tensor_tensor(out=out, in0=in0, in1=in1, op=op), reads=[in0, in1], writes=[out], cost=ecost(eng, nfree(out), two))

    def ts(eng, out, in0, s1, op0, s2=None, op1=None):
        rd = [in0] + [s for s in (s1, s2) if s is not None and not isinstance(s, (int, float))]
        c = ecost(eng, nfree(out))
        if op1 is None:
            P.op(eng, lambda e: e.tensor_scalar(out=out, in0=in0, scalar1=s1, scalar2=None, op0=op0), reads=rd, writes=[out], cost=c)
        else:
            P.op(eng, lambda e: e.tensor_scalar(out=out, in0=in0, scalar1=s1, scalar2=s2, op0=op0, op1=op1), reads=rd, writes=[out], cost=c)

    def stt(out, in0, scalar, in1, op0, op1):
        rd = [in0, in1] + ([] if isinstance(scalar, (int, float)) else [scalar])
        P.op("dve", lambda e: e.scalar_tensor_tensor(out=out, in0=in0, scalar=scalar, in1=in1, op0=op0, op1=op1), reads=rd, writes=[out], cost=ecost("dve", nfree(out), True))

    def scan(out, d0, d1, init):
        rd = [d0, d1] + ([] if isinstance(init, (int, float)) else [init])
        P.op("dve", lambda e: e.tensor_tensor_scan(out=out, data0=d0, data1=d1, initial=init, op0=ALU.mult, op1=ALU.add), reads=rd, writes=[out], cost=ecost("dve", nfree(out), True))

    def cp(eng, out, in_):
        if eng == "act":
            P.op("act", lambda e: e.copy(out=out, in_=in_), reads=[in_], writes=[out], cost=0.22 + nfree(out) / 1400.0)
        else:
            P.op(eng, lambda e: e.tensor_copy(out=out, in_=in_), reads=[in_], writes=[out], cost=ecost(eng, nfree(out)))

    def mset(eng, ap, val):
        P.op(eng, lambda e: e.memset(ap, val), reads=[], writes=[ap], cost=ecost(eng, nfree(ap)))

    def recip(out, in_):
        P.op("dve", lambda e: e.reciprocal(out=out, in_=in_), reads=[in_], writes=[out], cost=ecost("dve", nfree(out)))

    def sigm(out, in_, nbias=None):
        if nbias is None:
            act(out, in_, AF.Exp, scale=-1.0)
        else:
            act(out, in_, AF.Exp, scale=-1.0, bias=nbias)
        ts("dve", out, out, 1.0, ALU.add)
        recip(out, out)

    def VV(l, v, c):
        o = (l * NV + v) * 8 + c
        return vec[:, o:o + 1]

    P.dma("sp", cf[:], cst)
    P.dma("sp", vec[:], vecs)
    P.dma("pool", cb[:], cstb)
    for kc in range(8):
        P.dma("sp", xf[:, kc, :], xT[:, kc, :])
        P.dma("pool", xb[:, kc, :], xT[:, kc, :])
    vec4 = vec[:].rearrange("p (l v c) -> p l v c", l=L, v=NV)
    for l in range(L):
        act(vec4[:, l, V_CP, :], vec4[:, l, V_AP, :], AF.Exp)
        act(vec4[:, l, V_CP, :], vec4[:, l, V_CP, :], AF.Ln, bias=1.0)
        ts("dve", vec4[:, l, V_CP, :], vec4[:, l, V_CP, :], -8.0, ALU.mult)
        ts("dve", vec4[:, l, V_OMKA, :], vec4[:, l, V_KA, :], -1.0, ALU.mult, 1.0, ALU.add)
        for vs, vd in ((V_BA, V_NBA), (V_BX, V_NBX), (V_W0, V_NW0), (V_A0, V_NA0)):
            ts("dve", vec4[:, l, vd, :], vec4[:, l, vs, :], -1.0, ALU.mult)

    W = WB
    lx = sb("lx", 3 + W)
    lxs = sb("lxs", 16 * 11)
    u = sb("u", W); ubf = sb("ubf", W, BF16)
    gr = sb("gr", W); gi = sb("gi", W); mlt = sb("mlt", W); hh = sb("hh", W); hc = sb("hc", 1)
    h0s = sb("h0s", 16)
    gy = sb("gy", W); sga = sb("sga", W); oa = sb("oa", W)
    shp = [sb("shp%d" % q, 1 + W) for q in range(3)]
    shs = [sb("shs%d" % q, 16 * 9) for q in range(3)]
    shlp = shp
    shls = shs
    tmp = sb("tmp", W)
    mx = [sb("mx%d" % q, W) for q in range(3)]
    vbf = sb("vbf", W, BF16)
    sg = sb("sg", W); asig = sb("asig", W); gg = sb("gg", W)
    UF = sb("UF", 1024)
    kk, kk2, rn, bb, kh, cum, gam, ig = [UF[:, i * 128:(i + 1) * 128] for i in range(8)]
    e0 = sb("e0", W)
    Rt = sb("Rt", W, BF16); At = sb("At", W, BF16); Kt = sb("Kt", W, BF16); Bt = sb("Bt", W, BF16)
    Bh = sb("Bh", 128, BF16); Kh = sb("Kh", 128, BF16)
    rk = sb("rk", W)
    TT = sb("TT", 384, BF16)
    MX = sb("MX", 512, BF16); MY = sb("MY", 512, BF16); MZ = sb("MZ", 256, BF16)
    PP = [sb("PP%d" % i, 512, BF16) for i in range(2)]
    NN = [sb("NN%d" % i, 256, BF16) for i in range(2)]
    XTs = sb("XTs", 128, BF16); UTs = sb("UTs", 128, BF16)
    Sp = sb("Sp", 64); Spb = sb("Spb", 64, BF16)
    Ss = sb("Ss", 1024); Ssb = sb("Ssb", 1024, BF16)
    UB = sb("UB", 4096, BF16)
    Atj, Rtj, BhTj, KhTj = [UB[:, i * 1024:(i + 1) * 1024] for i in range(4)]
    WK0 = dict(MX=MX, MY=MY, MZ=MZ, PP=PP, NN=NN, TT=TT, XTs=XTs, UTs=UTs, Bh=Bh, Kh=Kh)
    WK1 = dict(MX=UB[:, 0:512], MY=UB[:, 512:1024], MZ=UB[:, 1024:1280],
               PP=[UB[:, 1280:1792], UB[:, 1792:2304]], NN=[UB[:, 2304:2560], UB[:, 2560:2816]],
               TT=UB[:, 2816:3200], XTs=UB[:, 3200:3328], UTs=UB[:, 3328:3456],
               Bh=UB[:, 3456:3584], Kh=UB[:, 3584:3712])
    of_ = sb("of", W); o2 = sb("o2", W); mus = sb("mus", W); var = sb("var", W)
    mrg = sb("mrg", W, BF16)
    st_shift_p = sb("st_shift_p", 26)
    st_lru_p = sb("st_lru_p", 8)
    scv = sb("scv", 48); scvo = sb("scvo", 48); sshl = sb("sshl", 416); slrl = sb("slrl", 128)
    st_shift_s = sshl
    st_lru_s = slrl
    oa2 = [oa, sb("oa_b", W)]; rk2 = [rk, sb("rk_b", W)]; gg2 = [gg, sb("gg_b", W)]
    sgb = sb("sgb", W); tmp2 = tmp
    gendc = sb("gendc", 2); gends = sb("gends", 16)
    lnq, lnm, lnr, hr = [UF[:, i * 256:(i + 1) * 256] for i in range(4)]
    hid = [UB[:, j * 256:(j + 1) * 256] for j in range(4)]
    print("SBUF bytes/partition:", tot[0])

    blocks = [(t0, W, "p") for t0 in range(0, NPR, W)] + [(NPR, 128, "s")]
    nblk = len(blocks)

    def v3(ap, ns):
        return ap.rearrange("p (s t) -> p s t", s=ns)

    def mixer_block(l, c, bi):
        t0, Wd, kind = blocks[bi]
        ns = 1 if kind == "p" else NSQ
        T = Wd // ns
        nch = Wd // 128
        last_p = (kind == "p" and bi == nblk - 2)
        wcc = wcb[c % 2]
        xbs = lambda kc: xb[:, kc, t0:t0 + Wd]
        oa, rk, gg = oa2[bi % 2], rk2[bi % 2], gg2[bi % 2]
        WK = WK1 if (kind == "p" and bi % 2 == 1) else WK0
        MX, MY, MZ, PP, NN, TT, XTs, UTs, Bh, Kh = (WK[k] for k in ("MX", "MY", "MZ", "PP", "NN", "TT", "XTs", "UTs", "Bh", "Kh"))

        def proj(j):
            b = bank()
            for kc in range(8):
                mm(b[:, 0:Wd], wcc[:, kc * 896 + j * 128:kc * 896 + (j + 1) * 128], xbs(kc), start=(kc == 0), stop=(kc == 7))
            return b[:, 0:Wd]

        role[0] = "a"
        p_lx = proj(0)
        if kind == "p":
            L3 = v3(lx[:, 0:3 + T], 1)
            if bi == 0:
                mset("pool", lx[:, 0:3], 0.0)
            else:
                Tp = blocks[bi - 1][1]
                cp("pool", tmp[:, 0:3], lx[:, Tp:Tp + 3])
                cp("pool", lx[:, 0:3], tmp[:, 0:3])
        else:
            L3 = v3(lxs[:, :], NSQ)
        cp("act", L3[:, :, 3:3 + T], v3(p_lx, ns))
        u3 = v3(u[:, 0:Wd], ns)
        ts("dve", u3, L3[:, :, 0:T], VV(l, V_CW + 0, c), ALU.mult, VV(l, V_CB, c), ALU.add)
        for j in range(1, 4):
            stt(u3, L3[:, :, j:j + T], VV(l, V_CW + j, c), u3, ALU.mult, ALU.add)
        if last_p:
            P.dma("sp", o_conv_p[l, c], lx[:, T:T + 3], is_output=True)
        if kind == "s":
            cp("pool", v3(scvo[:, :], NSQ), L3[:, :, 8:11])
            P.dma("sp", o_conv_s[l, c], scvo[:, :], is_output=True)
        cp("act", ubf[:, 0:Wd], u[:, 0:Wd])
        p_gr = bank()[:, 0:Wd]
        smc = smw[c % 2]
        mm(p_gr, smc[:, 256:384], ubf[:, 0:Wd])
        p_gi = bank()[:, 0:Wd]
        mm(p_gi, smc[:, 384:512], ubf[:, 0:Wd])
        sigm(gr[:, 0:Wd], p_gr, VV(l, V_NBA, c))
        sigm(gi[:, 0:Wd], p_gi, VV(l, V_NBX, c))
        act(gr[:, 0:Wd], gr[:, 0:Wd], AF.Exp, scale=VV(l, V_CP, c))
        act(mlt[:, 0:Wd], gr[:, 0:Wd], AF.Square)
        ts("pool", mlt[:, 0:Wd], mlt[:, 0:Wd], -1.0, ALU.mult, 1.0, ALU.add)
        act(mlt[:, 0:Wd], mlt[:, 0:Wd], AF.Ln)
        act(mlt[:, 0:Wd], mlt[:, 0:Wd], AF.Exp, scale=0.5)
        if kind == "p" and bi == 0:
            mset("pool", mlt[:, 0:1], 1.0)
        tt("dve", gi[:, 0:Wd], gi[:, 0:Wd], u[:, 0:Wd], ALU.mult)
        tt("dve", gi[:, 0:Wd], gi[:, 0:Wd], mlt[:, 0:Wd], ALU.mult)
        if kind == "p":
            if bi == 0:
                scan(hh[:, 0:Wd], gr[:, 0:Wd], gi[:, 0:Wd], 0.0)
            else:
                scan(hh[:, 0:Wd], gr[:, 0:Wd], gi[:, 0:Wd], hc[:, 0:1])
            cp("pool", hc[:, 0:1], hh[:, Wd - 1:Wd])
            if last_p:
                cp("pool", st_lru_p[:, c:c + 1], hh[:, Wd - 1:Wd])
        else:
            for j in range(NSQ):
                scan(hh[:, j * 8:(j + 1) * 8], gr[:, j * 8:(j + 1) * 8], gi[:, j * 8:(j + 1) * 8], h0s[:, j:j + 1])
            cp("pool", st_lru_s[:, c * 16:(c + 1) * 16].unsqueeze(2), v3(hh[:, 0:Wd], NSQ)[:, :, 7:8])
        p_ly = proj(1)
        act(gy[:, 0:Wd], p_ly, AF.Gelu_apprx_tanh)
        p_ga = proj(5)
        sigm(sga[:, 0:Wd], p_ga)
        tt("dve", oa[:, 0:Wd], hh[:, 0:Wd], gy[:, 0:Wd], ALU.mult)
        tt("pool", oa[:, 0:Wd], oa[:, 0:Wd], sga[:, 0:Wd], ALU.mult)

        for q in range(3):
            pq = proj(2 + q)
            if kind == "p":
                S3 = v3(shp[q][:, 0:1 + T], 1)
                if bi == 0:
                    mset("pool", shp[q][:, 0:1], 0.0)
                else:
                    Tp = blocks[bi - 1][1]
                    cp("pool", shp[q][:, 0:1], shp[q][:, Tp:Tp + 1])
            else:
                S3 = v3(shs[q][:, :], NSQ)
            cp("act", S3[:, :, 1:1 + T], v3(pq, ns))
            t3 = v3(tmp2[:, 0:Wd], ns)
            tt("pool", t3, S3[:, :, 0:T], S3[:, :, 1:1 + T], ALU.subtract)
            stt(v3(mx[q][:, 0:Wd], ns), t3, VV(l, V_MUR + q, c), S3[:, :, 1:1 + T], ALU.mult, ALU.add)
            if last_p:
                cp("pool", st_shift_p[:, q * 8 + c:q * 8 + c + 1], shp[q][:, T:T + 1])
            if kind == "s":
                o = (q * 8 + c) * 16
                cp("pool", st_shift_s[:, o:o + 16].unsqueeze(2), S3[:, :, 8:9])
        r_m, k_m, v_m = mx[0][:, 0:Wd], mx[1][:, 0:Wd], mx[2][:, 0:Wd]
        cs = slice(c * 128, (c + 1) * 128)
        p_d = bank()[:, 0:Wd]
        mm(p_d, smc[0:64, 0:128], loraA[0:64, t0:t0 + Wd])
        p_a = bank()[:, 0:Wd]
        mm(p_a, smc[64:128, 0:128], loraA[64:128, t0:t0 + Wd])
        p_g = bank()[:, 0:Wd]
        mm(p_g, smc[:, 128:256], loraB[:, t0:t0 + Wd])
        sigm(sg[:, 0:Wd], p_d, VV(l, V_NW0, c))
        sigm(asig[:, 0:Wd], p_a, VV(l, V_NA0, c))
        cp("act", gg[:, 0:Wd], p_g)
        ts("pool", kk[:, 0:Wd], k_m, VV(l, V_KK, c), ALU.mult)
        act(kk2[:, 0:Wd], kk[:, 0:Wd], AF.Square)
        p_n = bank()[:, 0:Wd]
        mm(p_n, cf[:, C_BONE:C_BONE + 128], kk2[:, 0:Wd])
        ts("dve", rn[:, 0:Wd], p_n, 1e-24, ALU.max)
        act(rn[:, 0:Wd], rn[:, 0:Wd], AF.Ln)
        act(rn[:, 0:Wd], rn[:, 0:Wd], AF.Exp, scale=-0.5)
        tt("dve", kk[:, 0:Wd], kk[:, 0:Wd], rn[:, 0:Wd], ALU.mult)
        tt("dve", bb[:, 0:Wd], kk[:, 0:Wd], asig[:, 0:Wd], ALU.mult)
        ts("dve", kh[:, 0:Wd], asig[:, 0:Wd], VV(l, V_KA, c), ALU.mult, VV(l, V_OMKA, c), ALU.add)
        tt("dve", kh[:, 0:Wd], kh[:, 0:Wd], k_m, ALU.mult)
        rmask = cf[:, C_RMS:C_RMS + 128] if kind == "s" else None
        if kind == "p":
            for q in range(nch):
                scan(cum[:, q * 128:(q + 1) * 128], cf[:, C_RMP:C_RMP + 128], sg[:, q * 128:(q + 1) * 128], 0.0)
        else:
            scan(cum[:, 0:Wd], rmask, sg[:, 0:Wd], 0.0)
        act(gam[:, 0:Wd], cum[:, 0:Wd], AF.Exp, scale=-C0)
        act(ig[:, 0:Wd], cum[:, 0:Wd], AF.Exp, scale=C0)
        tt("pool", tmp[:, 0:Wd], cum[:, 0:Wd], sg[:, 0:Wd], ALU.subtract)
        act(e0[:, 0:Wd], tmp[:, 0:Wd], AF.Exp, scale=-C0)
        tt("dve", Rt[:, 0:Wd], r_m, gam[:, 0:Wd], ALU.mult)
        stt(At[:, 0:Wd], kk[:, 0:Wd], -1.0, e0[:, 0:Wd], ALU.mult, ALU.mult)
        tt("pool", Kt[:, 0:Wd], kh[:, 0:Wd], ig[:, 0:Wd], ALU.mult)
        tt("pool", Bt[:, 0:Wd], bb[:, 0:Wd], ig[:, 0:Wd], ALU.mult)
        stt(rk[:, 0:Wd], r_m, VV(l, V_RK, c), kh[:, 0:Wd], ALU.mult, ALU.mult)
        p_bon = bank()[:, 0:Wd]
        mm(p_bon, cf[:, C_BONE:C_BONE + 128], rk[:, 0:Wd])
        tt("dve", rk[:, 0:Wd], p_bon, v_m, ALU.mult)
        cp("act", vbf[:, 0:Wd], v_m)

        role[0] = "b"
        S_f = Sp if kind == "p" else Ss
        S_b = Spb if kind == "p" else Ssb
        msu, mui, msl = (C_SU, C_UI, C_SL) if kind == "p" else (C_SSU, C_SUI, C_SSL)
        for q in range(nch):
            ck = slice(q * 128, (q + 1) * 128)
            first = (kind == "p" and bi == 0 and q == 0)
            if kind == "p":
                gend = gendc[:, bi % 2:bi % 2 + 1]
                cp("pool", gend, gam[:, q * 128 + 127:q * 128 + 128])
                ts("dve", Bh[:, :], Bt[:, ck], gend, ALU.mult)
                ts("dve", Kh[:, :], Kt[:, ck], gend, ALU.mult)
            else:
                cp("pool", gends[:, :].unsqueeze(2), v3(gam[:, 0:128], NSQ)[:, :, 7:8])
                g3 = gends[:, :].unsqueeze(2)
                tt("dve", v3(Bh[:, :], NSQ), v3(Bt[:, ck], NSQ), g3.to_broadcast([128, NSQ, 8]), ALU.mult)
                tt("dve", v3(Kh[:, :], NSQ), v3(Kt[:, ck], NSQ), g3.to_broadcast([128, NSQ, 8]), ALU.mult)
            tr(pst[:, 0:128], vbf[:, ck])
            tr(pst[:, 128:256], Bh[:, :])
            tr(pst[:, 256:384], Kh[:, :])
            cp("act", TT[:, 0:384], pst[:, 0:384])
            vT, BhT, KhT = TT[:, 0:128], TT[:, 128:256], TT[:, 256:384]
            bX = bank(); bY = bank(); bZ = bank()
            for hp in range(2):
                hs = slice(hp * 64, (hp + 1) * 64)
                mm(bX[:, hp * 128:(hp + 1) * 128], Bt[hs, ck], At[hs, ck])
                mm(bX[:, (2 + hp) * 128:(3 + hp) * 128], Kt[hs, ck], At[hs, ck])
                mm(bY[:, hp * 128:(hp + 1) * 128], Bt[hs, ck], Rt[hs, ck])
                mm(bY[:, (2 + hp) * 128:(3 + hp) * 128], Kt[hs, ck], Rt[hs, ck])
                mm(bZ[:, hp * 128:(hp + 1) * 128], At[hs, ck], Bt[hs, ck])
            m4 = lambda o: cb[:, o:o + 128].unsqueeze(1).to_broadcast([128, 4, 128])
            m2 = lambda o: cb[:, o:o + 128].unsqueeze(1).to_broadcast([128, 2, 128])
            r4 = lambda a: a.rearrange("p (a b) -> p a b", b=128)
            tt("dve", r4(MX[:, :]), r4(bX[:, 0:512]), m4(msu), ALU.mult)
            tt("dve", r4(MY[:, :]), r4(bY[:, 0:512]), m4(mui), ALU.mult)
            tt("dve", r4(MZ[:, :]), r4(bZ[:, 0:256]), m2(msl), ALU.mult)
            nlev = 6 if kind == "p" else 2
            cur = 0
            cp("pool", PP[0][:, 0:256], MX[:, 0:256])
            cp("pool", PP[0][:, 256:512], MZ[:, 0:256])
            tt("pool", r4(NN[0][:, :]), r4(MX[:, 0:256]), m2(C_ID), ALU.add)
            ncur = 0
            for i in range(nlev):
                Pc = PP[cur]
                bQ = bank()
                for hp in range(2):
                    hcs = slice(hp * 128, (hp + 1) * 128)
                    hts = slice(256 + hp * 128, 256 + (hp + 1) * 128)
                    if i < nlev - 1:
                        mm(bQ[:, hcs], Pc[:, hts], Pc[:, hcs])
                    mm(bQ[:, hts], Pc[:, hcs], Pc[:, hts])
                if i >= 1:
                    bR = bank()
                    for hp in range(2):
                        hcs = slice(hp * 128, (hp + 1) * 128)
                        hts = slice(256 + hp * 128, 256 + (hp + 1) * 128)
                        mm(bR[:, hcs], Pc[:, hts], NN[ncur][:, hcs])
                    tt("dve", NN[1 - ncur][:, :], bR[:, 0:256], NN[ncur][:, :], ALU.add)
                    ncur = 1 - ncur
                nxt = 1 - cur
                if i < nlev - 1:
                    cp("act", PP[nxt][:, :], bQ[:, 0:512])
                else:
                    cp("act", PP[nxt][:, 256:512], bQ[:, 256:512])
                cur = nxt
            bR = bank()
            for hp in range(2):
                hcs = slice(hp * 128, (hp + 1) * 128)
                hts = slice(256 + hp * 128, 256 + (hp + 1) * 128)
                mm(bR[:, hcs], PP[cur][:, hts], NN[ncur][:, hcs])
            tt("dve", NN[1 - ncur][:, :], bR[:, 0:256], NN[ncur][:, :], ALU.add)
            ncur = 1 - ncur
            Nf = NN[ncur]
            if kind == "s":
                b3 = lambda a: a.unsqueeze(1).to_broadcast([128, 8, 128])
                j3 = lambda a: a.rearrange("p (j s) -> p j s", j=8)

                def smf(half):
                    return cb[:, C_SMF + half * 1024:C_SMF + (half + 1) * 1024].rearrange("p (j s) -> p j s", j=8)

                def smt(half):
                    return cb[:, C_SMT + half * 8:C_SMT + (half + 1) * 8].unsqueeze(2).to_broadcast([128, 8, 128])
            bXT = bank()
            bXT2 = [bXT, bank()] if kind == "s" else None
            for hp in range(2):
                hs = slice(hp * 64, (hp + 1) * 64)
                o = bXT[:, hp * 64:(hp + 1) * 64] if kind == "p" else bXT2[hp][:, 0:64]
                if kind == "p":
                    if not first:
                        mm(o, At[hs, ck], S_b[hs, 0:64], start=True, stop=False)
                mm(o, MX[:, (2 + hp) * 128:(3 + hp) * 128], vT[:, hs], start=(first or kind == "s"), stop=(kind == "p"))
            if kind == "s":
                for half in range(2):
                    tt("pool", j3(Atj[:, :]), b3(At[:, ck]), smf(half), ALU.mult)
                    for hp in range(2):
                        hs = slice(hp * 64, (hp + 1) * 64)
                        o = bXT2[hp][:, 0:64]
                        for jj in range(8):
                            j = half * 8 + jj
                            mm(o, Atj[hs, jj * 128:(jj + 1) * 128], S_b[hs, j * 64:(j + 1) * 64], start=False, stop=(j == NSQ - 1))
            if kind == "p":
                cp("act", XTs[:, :], bXT[:, 0:128])
            else:
                cp("act", XTs[:, 0:64], bXT2[0][:, 0:64])
                cp("act", XTs[:, 64:128], bXT2[1][:, 0:64])
            bUT = bank()
            for hp in range(2):
                mm(bUT[:, hp * 64:(hp + 1) * 64], Nf[:, hp * 128:(hp + 1) * 128], XTs[:, hp * 64:(hp + 1) * 64])
            cp("act", UTs[:, :], bUT[:, 0:128])
            bO = bank()
            for hp in range(2):
                hs = slice(hp * 64, (hp + 1) * 64)
                o = bO[hs, 0:128]
                if kind == "p":
                    if not first:
                        mm(o, S_b[hs, 0:64], Rt[hs, ck], start=True, stop=False)
                mm(o, UTs[:, hs], MY[:, hp * 128:(hp + 1) * 128], start=(first or kind == "s"), stop=False)
                mm(o, vT[:, hs], MY[:, (2 + hp) * 128:(3 + hp) * 128], start=False, stop=(kind == "p"))
            if kind == "s":
                for half in range(2):
                    tt("pool", j3(Rtj[:, :]), b3(Rt[:, ck]), smf(half), ALU.mult)
                    for hp in range(2):
                        hs = slice(hp * 64, (hp + 1) * 64)
                        o = bO[hs, 0:128]
                        for jj in range(8):
                            j = half * 8 + jj
                            mm(o, S_b[hs, j * 64:(j + 1) * 64], Rtj[hs, jj * 128:(jj + 1) * 128], start=False, stop=(j == NSQ - 1))
            cp("act", of_[:, ck], bO[:, 0:128])
            if kind == "p":
                bS = bank()
                for hp in range(2):
                    hs = slice(hp * 64, (hp + 1) * 64)
                    mm(bS[hs, 0:64], BhT[:, hs], UTs[:, hs], start=True, stop=False)
                    mm(bS[hs, 0:64], KhT[:, hs], vT[:, hs], start=False, stop=True)
                if first:
                    cp("dve", S_f[:, :], bS[:, 0:64])
                else:
                    stt(S_f[:, :], S_f[:, :], gend, bS[:, 0:64], ALU.mult, ALU.add)
                cp("act", S_b[:, :], S_f[:, :])
            else:
                gb = gends[:, :].unsqueeze(2).to_broadcast([128, NSQ, 64])
                tt("dve", v3(S_f[:, :], NSQ), v3(S_f[:, :], NSQ), gb, ALU.mult)
                for half in range(2):
                    tt("pool", j3(BhTj[:, :]), b3(BhT), smt(half), ALU.mult)
                    tt("pool", j3(KhTj[:, :]), b3(KhT), smt(half), ALU.mult)
                    bS = bank()
                    for hp in range(2):
                        hs = slice(hp * 64, (hp + 1) * 64)
                        for jj in range(8):
                            mm(bS[hs, jj * 64:(jj + 1) * 64], BhTj[:, jj * 128 + hp * 64:jj * 128 + (hp + 1) * 64], UTs[:, hs], start=True, stop=False)
                            mm(bS[hs, jj * 64:(jj + 1) * 64], KhTj[:, jj * 128 + hp * 64:jj * 128 + (hp + 1) * 64], vT[:, hs], start=False, stop=True)
                    tt("dve", S_f[:, half * 512:(half + 1) * 512], S_f[:, half * 512:(half + 1) * 512], bS[:, 0:512], ALU.add)
        if last_p:
            P.dma("sp", o_wkv_p[l, c], Sp[:, :], is_output=True)
        if kind == "s":
            P.dma("sp", o_wkv_s[l, c], Ss[:, :], is_output=True)

        act(o2[:, 0:Wd], of_[:, 0:Wd], AF.Square)
        p_mu = bank()[:, 0:Wd]
        mm(p_mu, cf[:, C_BONE64:C_BONE64 + 128], of_[:, 0:Wd])
        p_m2 = bank()[:, 0:Wd]
        mm(p_m2, cf[:, C_BONE64:C_BONE64 + 128], o2[:, 0:Wd])
        cp("act", mus[:, 0:Wd], p_mu)
        act(var[:, 0:Wd], p_mu, AF.Square)
        tt("dve", var[:, 0:Wd], p_m2, var[:, 0:Wd], ALU.subtract)
        act(var[:, 0:Wd], var[:, 0:Wd], AF.Ln, bias=cf_eps_gn[:, 0:1])
        act(var[:, 0:Wd], var[:, 0:Wd], AF.Exp, scale=-0.5)
        tt("dve", of_[:, 0:Wd], of_[:, 0:Wd], mus[:, 0:Wd], ALU.subtract)
        tt("dve", of_[:, 0:Wd], of_[:, 0:Wd], var[:, 0:Wd], ALU.mult)
        ts("dve", of_[:, 0:Wd], of_[:, 0:Wd], VV(l, V_GW, c), ALU.mult, VV(l, V_GB, c), ALU.add)
        tt("pool", of_[:, 0:Wd], of_[:, 0:Wd], rk[:, 0:Wd], ALU.add)
        tt("pool", of_[:, 0:Wd], of_[:, 0:Wd], gg[:, 0:Wd], ALU.mult)
        p_gb = proj(6)
        sigm(sgb[:, 0:Wd], p_gb)
        tt("dve", of_[:, 0:Wd], of_[:, 0:Wd], sgb[:, 0:Wd], ALU.mult)
        tt("dve", mrg[:, 0:Wd], of_[:, 0:Wd], oa[:, 0:Wd], ALU.add)
        wo_c = wob[c % 2]
        for oc in range(8):
            b = bank()[:, 0:Wd]
            mm(b, wo_c[:, oc * 128:(oc + 1) * 128], mrg[:, 0:Wd])
            xs = xf[:, oc, t0:t0 + Wd]
            if c == 0:
                stt(xs, xs, ALPHA, b, ALU.mult, ALU.add)
            else:
                tt("dve", xs, b, xs, ALU.add)
        role[0] = "all"

    cf_eps_gn = sb("epsgn", 1)
    cf_eps_ln = sb("epsln", 1)
    mset("pool", cf_eps_gn[:, :], GN_EPS)
    mset("pool", cf_eps_ln[:, :], LN_EPS)

    def layer_norm(l, vg, vb):
        for t0 in range(0, NTOK, 256):
            Wd = min(256, NTOK - t0)
            p_s = bank()[:, 0:Wd]
            for oc in range(8):
                mm(p_s, cf[:, C_ONESD:C_ONESD + 128], xf[:, oc, t0:t0 + Wd], start=(oc == 0), stop=(oc == 7))
            p_q = bank()[:, 0:Wd]
            for oc in range(8):
                act(lnq[:, 0:Wd], xf[:, oc, t0:t0 + Wd], AF.Square)
                mm(p_q, cf[:, C_ONESD:C_ONESD + 128], lnq[:, 0:Wd], start=(oc == 0), stop=(oc == 7))
            cp("act", lnm[:, 0:Wd], p_s)
            act(lnr[:, 0:Wd], p_s, AF.Square)
            tt("dve", lnr[:, 0:Wd], p_q, lnr[:, 0:Wd], ALU.subtract)
            act(lnr[:, 0:Wd], lnr[:, 0:Wd], AF.Sqrt, bias=cf_eps_ln[:, 0:1])
            recip(lnr[:, 0:Wd], lnr[:, 0:Wd])
            for oc in range(8):
                xs = xf[:, oc, t0:t0 + Wd]
                tt("dve", xs, xs, lnm[:, 0:Wd], ALU.subtract)
                tt("dve", xs, xs, lnr[:, 0:Wd], ALU.mult)
                ts("dve", xs, xs, VV(l, vg, oc), ALU.mult, VV(l, vb, oc), ALU.add)
                cp("act", xb[:, oc, t0:t0 + Wd], xs)

    for l in range(NL_RUN):
        P.dma("pool", wlb, wl[l])
        P.dma("pool", wcb[0], wc[l, 0])
        P.dma("pool", wob[0][:], wo[l, 0])
        P.dma("pool", smw[0][:], smallw[l, 0])
        P.dma("sp", sshl[:, :], sshift[l])
        P.dma("sp", slrl[:, :], slru[l])
        for q in range(2):
            cp("pool", v3(shls[q][:, :], NSQ)[:, :, 0:1], sshl[:, (24 + q) * 16:(25 + q) * 16].unsqueeze(2))
        for bi, (t0, Wd, kind) in enumerate(blocks):
            ns = 1 if kind == "p" else NSQ
            T = Wd // ns
            for q in range(2):
                b = bank()[:, 0:Wd]
                for kc in range(8):
                    mm(b, wlb[:, kc * 256 + q * 128:kc * 256 + (q + 1) * 128], xb[:, kc, t0:t0 + Wd], start=(kc == 0), stop=(kc == 7))
                if kind == "p":
                    S3 = v3(shlp[q][:, 0:1 + T], 1)
                    if bi == 0:
                        mset("pool", shlp[q][:, 0:1], 0.0)
                    else:
                        Tp = blocks[bi - 1][1]
                        cp("pool", shlp[q][:, 0:1], shlp[q][:, Tp:Tp + 1])
                else:
                    S3 = v3(shls[q][:, :], NSQ)
                cp("act", S3[:, :, 1:1 + T], v3(b, ns))
                t3 = v3(tmp[:, 0:Wd], ns)
                tt("pool", t3, S3[:, :, 0:T], S3[:, :, 1:1 + T], ALU.subtract)
                stt(v3(u[:, 0:Wd], ns), t3, VV(l, V_MUL, q), S3[:, :, 1:1 + T], ALU.mult, ALU.add)
                if q == 0:
                    act(loraA[0:64, t0:t0 + Wd], u[0:64, 0:Wd], AF.Tanh)
                    cp("act", loraA[64:128, t0:t0 + Wd], u[64:128, 0:Wd])
                else:
                    act(loraB[:, t0:t0 + Wd], u[:, 0:Wd], AF.Sigmoid)
                if kind == "p" and bi == nblk - 2:
                    cp("pool", st_shift_p[:, 24 + q:25 + q], shlp[q][:, T:T + 1])
                if kind == "s":
                    cp("pool", st_shift_s[:, (24 + q) * 16:(25 + q) * 16].unsqueeze(2), S3[:, :, 8:9])
        for c in range(DBG_NC):
            if c + 1 < DBG_NC:
                P.dma("pool", wcb[(c + 1) % 2], wc[l, c + 1])
                P.dma("pool", wob[(c + 1) % 2][:], wo[l, c + 1])
                P.dma("pool", smw[(c + 1) % 2][:], smallw[l, c + 1])
            P.dma("sp", scv[:, :], sconv[l, c])
            cp("pool", v3(lxs[:, :], NSQ)[:, :, 0:3], v3(scv[:, :], NSQ))
            cp("pool", h0s[:, :], slrl[:, c * 16:(c + 1) * 16])
            for q in range(3):
                o = (q * 8 + c) * 16
                cp("pool", v3(shs[q][:, :], NSQ)[:, :, 0:1], sshl[:, o:o + 16].unsqueeze(2))
            P.dma("sp", Ss[:, :], swkv[l, c])
            cp("pool", Ssb[:, :], Ss[:, :])
            for bi in (range(nblk) if DBG_NB is None else DBG_NB):
                mixer_block(l, c, bi)
        P.dma("sp", o_lru_p[l], st_lru_p[:, :], is_output=True)
        P.dma("sp", o_lru_s[l], st_lru_s[:, :], is_output=True)
        P.dma("sp", o_shift_p[l], st_shift_p[:, :], is_output=True)
        P.dma("sp", o_shift_s[l], st_shift_s[:, :], is_output=True)
        if not DBG_POST:
            continue
        layer_norm(l, V_L1G, V_L1B)
        for e8 in range(8):
            P.dma("pool", w1b[e8 % 2], w1[l, e8])
            P.dma("pool", w2b[e8 % 2], w2[l, e8])
            for t0 in range(0, NTOK, 256):
                Wd = min(256, NTOK - t0)
                for jj in range(4):
                    b = bank()[:, 0:Wd]
                    for kc in range(8):
                        mm(b, w1b[e8 % 2][:, kc * 512 + jj * 128:kc * 512 + (jj + 1) * 128], xb[:, kc, t0:t0 + Wd], start=(kc == 0), stop=(kc == 7))
                    act(hr[:, 0:Wd], b, AF.Relu)
                    act(hid[jj][:, 0:Wd], hr[:, 0:Wd], AF.Square)
                for oc in range(8):
                    b = bank()[:, 0:Wd]
                    for jj in range(4):
                        mm(b, w2b[e8 % 2][:, jj * 1024 + oc * 128:jj * 1024 + (oc + 1) * 128], hid[jj][:, 0:Wd], start=(jj == 0), stop=(jj == 3))
                    xs = xf[:, oc, t0:t0 + Wd]
                    if e8 == 0:
                        stt(xs, xs, ALPHA, b, ALU.mult, ALU.add)
                    else:
                        tt("dve", xs, b, xs, ALU.add)
        layer_norm(l, V_L2G, V_L2B)
    for kc in range(8):
        P.dma("sp", yT[:, kc, :], xf[:, kc, :], is_output=True)
    P.emit()
    st.close()
    print("ops:", len(P.all), "sim_time_us:", getattr(P, "sim_time", None))
    return nc


def _prep_shared(inp):
    f = lambda k: np.asarray(inp[k], np.float32)
    w_in = f("w_in")
    V = np.zeros((L, NV, 1024), np.float32)
    V[:, 0:4] = f("conv_w")
    V[:, V_CB] = f("conv_b")
    V[:, V_BA] = f("lru_ba").reshape(L, 1024)
    V[:, V_BX] = f("lru_bx").reshape(L, 1024)
    V[:, V_AP] = f("lru_a_param")
    V[:, V_W0] = f("w0"); V[:, V_A0] = f("a0"); V[:, V_KK] = f("k_k"); V[:, V_KA] = f("k_a")
    V[:, V_RK] = f("r_k").reshape(L, 1024)
    V[:, V_GW] = f("gn_w"); V[:, V_GB] = f("gn_b")
    V[:, V_L1G] = f("ln1_g"); V[:, V_L1B] = f("ln1_b"); V[:, V_L2G] = f("ln2_g"); V[:, V_L2B] = f("ln2_b")
    mu = f("shift_mu")
    V[:, V_MUR] = mu[:, 0:1024]; V[:, V_MUR + 1] = mu[:, 1024:2048]; V[:, V_MUR + 2] = mu[:, 2048:3072]
    V[:, V_MUL, 0:256] = mu[:, 3072:3328]
    vecs = np.ascontiguousarray(V.reshape(L, NV, 8, 128).transpose(3, 0, 1, 2).reshape(128, L * NV * 8))
    offs = [0, 1024, 2048, 3072, 4096, 5376, 6400]
    wc = np.empty((L, 8, 128, 8 * 896), np.float32)
    for c in range(8):
        cols = np.concatenate([np.arange(o + c * 128, o + (c + 1) * 128) for o in offs])
        sel = w_in[:, :, cols]
        wc[:, c] = sel.reshape(L, 8, 128, 896).transpose(0, 2, 1, 3).reshape(L, 128, 8 * 896)
    wl = np.ascontiguousarray(w_in[:, :, 5120:5376].reshape(L, 8, 128, 256).transpose(0, 2, 1, 3).reshape(L, 128, 2048))
    wo = np.ascontiguousarray(f("w_out").reshape(L, 8, 128, 1024))
    upab = np.ascontiguousarray(np.concatenate([f("decay_up"), f("aaa_up")], axis=1))
    upg = np.ascontiguousarray(f("gate_up"))
    lruw = np.zeros((L, 128, 2, 8, 128), np.float32)
    for g, key in enumerate(("lru_wa", "lru_wx")):
        wg = f(key)
        for c in range(8):
            for hp in range(2):
                lruw[:, hp * 64:(hp + 1) * 64, g, c, hp * 64:(hp + 1) * 64] = wg[:, 2 * c + hp]
    smallw = np.empty((L, 8, 128, 512), np.float32)
    for c in range(8):
        smallw[:, c, :, 0:128] = upab[:, :, c * 128:(c + 1) * 128]
        smallw[:, c, :, 128:256] = upg[:, :, c * 128:(c + 1) * 128]
        smallw[:, c, :, 256:384] = lruw[:, :, 0, c, :]
        smallw[:, c, :, 384:512] = lruw[:, :, 1, c, :]
    w1 = np.ascontiguousarray(f("mlp_w1").reshape(L, 8, 128, 8, 512).transpose(0, 3, 2, 1, 4).reshape(L, 8, 128, 4096))
    w2 = np.ascontiguousarray(f("mlp_w2").reshape(L, 8, 4, 128, 1024).transpose(0, 1, 3, 2, 4).reshape(L, 8, 128, 4096))
    cc, cd = make_consts()
    return dict(vecs=vecs, cst=cc, cstb=cd, wc=wc, wl=wl, wo=wo, smallw=smallw, w1=w1, w2=w2)


_NC_CACHE = {}


def kernel(**inp):
    f = lambda k: np.asarray(inp[k], np.float32)
    shared = _prep_shared(inp)
    xp, xs = f("x_prompt"), f("x_sample")
    s_conv, s_lru, s_shift, s_wkv = f("state_conv"), f("state_lru"), f("state_shift"), f("state_wkv")
    in_maps = []
    for core in range(8):
        cs = slice(core * 16, (core + 1) * 16)
        xtok = np.concatenate([xp[core], xs[cs].reshape(128, 1024)], axis=0)
        m = dict(shared)
        m["xT"] = np.ascontiguousarray(xtok.T.reshape(8, 128, NTOK).transpose(1, 0, 2))
        m["sconv"] = np.ascontiguousarray(s_conv[:, cs].reshape(L, 16, 3, 8, 128).transpose(0, 3, 4, 1, 2).reshape(L, 8, 128, 48))
        m["slru"] = np.ascontiguousarray(s_lru[:, cs].reshape(L, 16, 8, 128).transpose(0, 3, 2, 1).reshape(L, 128, 128))
        m["sshift"] = np.ascontiguousarray(s_shift[:, cs].reshape(L, 16, 26, 128).transpose(0, 3, 2, 1).reshape(L, 128, 416))
        m["swkv"] = np.ascontiguousarray(s_wkv[:, cs].reshape(L, 16, 8, 2, 64, 64).transpose(0, 2, 3, 5, 1, 4).reshape(L, 8, 128, 1024))
        in_maps.append(m)
    if "nc" not in _NC_CACHE:
        _NC_CACHE["nc"] = build_program()
    res = run_bass_kernel_spmd(_NC_CACHE["nc"], in_maps, core_ids=list(range(8)))
    R = res.results
    y_p = np.empty((8, NPR, 1024), np.float32); y_s = np.empty((128, 8, 1024), np.float32)
    conv_p = np.empty((L, 8, 3, 1024), np.float32); lru_p = np.empty((L, 8, 1024), np.float32)
    shift_p = np.empty((L, 8, 3328), np.float32); wkv_p = np.empty((L, 8, 16, 64, 64), np.float32)
    conv_s = np.empty((L, 128, 3, 1024), np.float32); lru_s = np.empty((L, 128, 1024), np.float32)
    shift_s = np.empty((L, 128, 3328), np.float32); wkv_s = np.empty((L, 128, 16, 64, 64), np.float32)
    for core in range(8):
        r = R[core]
        cs = slice(core * 16, (core + 1) * 16)
        ytok = np.asarray(r["yT"]).reshape(128, 8, NTOK).transpose(2, 1, 0).reshape(NTOK, 1024)
        y_p[core] = ytok[:NPR]
        y_s[cs] = ytok[NPR:].reshape(16, 8, 1024)
        conv_p[:, core] = np.asarray(r["o_conv_p"]).reshape(L, 8, 128, 3).transpose(0, 3, 1, 2).reshape(L, 3, 1024)
        lru_p[:, core] = np.asarray(r["o_lru_p"]).reshape(L, 128, 8).transpose(0, 2, 1).reshape(L, 1024)
        shift_p[:, core] = np.asarray(r["o_shift_p"]).reshape(L, 128, 26).transpose(0, 2, 1).reshape(L, 3328)
        wkv_p[:, core] = np.asarray(r["o_wkv_p"]).reshape(L, 8, 2, 64, 64).transpose(0, 1, 2, 4, 3).reshape(L, 16, 64, 64)
        conv_s[:, cs] = np.asarray(r["o_conv_s"]).reshape(L, 8, 128, 16, 3).transpose(0, 3, 4, 1, 2).reshape(L, 16, 3, 1024)
        lru_s[:, cs] = np.asarray(r["o_lru_s"]).reshape(L, 128, 8, 16).transpose(0, 3, 2, 1).reshape(L, 16, 1024)
        shift_s[:, cs] = np.asarray(r["o_shift_s"]).reshape(L, 128, 26, 16).transpose(0, 3, 2, 1).reshape(L, 16, 3328)
        wkv_s[:, cs] = np.asarray(r["o_wkv_s"]).reshape(L, 8, 2, 64, 16, 64).transpose(0, 4, 1, 2, 5, 3).reshape(L, 16, 16, 64, 64)
    return (y_p, y_s, conv_p, lru_p, shift_p, wkv_p, conv_s, lru_s, shift_s, wkv_s)
```

```python
import heapq
import numpy as np
import concourse.bass as bass
import concourse.mybir as mybir

F32 = mybir.dt.float32
BF16 = mybir.dt.bfloat16
AF = mybir.ActivationFunctionType
ALU = mybir.AluOpType

ENGS = ("pe", "act", "dve", "pool", "sp")
N_DMA_SEMS = 12
EPOCH = 30000
SCHEDULE = True
SEM_LAT = 0.35
DEBUG_LINES = False
DBG_FREE = [0, 0]
CHAIN = {}


def region_of(ap):
    pat = ap.ap
    pitch, npart = pat[0]
    off = int(ap.offset)
    if pitch == 0:
        pitch = 1 << 40
    p_lo = off // pitch
    f_lo = off % pitch
    span = 0
    for step, cnt in pat[1:]:
        span += (cnt - 1) * abs(step)
    return (ap.name, p_lo, p_lo + npart, f_lo, f_lo + span + 1)


def _overlap(a, b):
    return a[1] < b[2] and b[1] < a[2] and a[3] < b[4] and b[3] < a[4]


def _covers(a, b):
    return a[1] <= b[1] and a[2] >= b[2] and a[3] <= b[3] and a[4] >= b[4]


def _onchip(ap):
    return ap is not None and str(getattr(ap, "space", "")) in ("SB", "PSUM")


class Op:
    __slots__ = ("id", "eng", "fn", "preds", "nsucc", "succs", "is_dma", "occ", "lat", "prio", "line",
                 "start", "fin", "sem", "val", "waits", "npend", "is_out", "mode")


class Prog:
    def __init__(self, nc):
        self.nc = nc
        self.all = []
        self.live = {}
        self.liver = {}
        self.pe_bank = {}

    def _track(self, o, reads, writes):
        preds = set()
        for ap in reads:
            if not _onchip(ap):
                continue
            r = region_of(ap)
            for ent in self.live.get(r[0], ()):
                if ent[1] == "w" and _overlap(ent[0], r):
                    preds.add(ent[2])
        for ap in writes:
            if not _onchip(ap):
                continue
            r = region_of(ap)
            for ent in self.live.get(r[0], ()):
                if _overlap(ent[0], r):
                    preds.add(ent[2])
            for ent in self.liver.get(r[0], ()):
                if _overlap(ent[0], r):
                    preds.add(ent[2])
            if o.eng == "pe":
                key = (r[0], r[3] // (512 if ap.dtype == F32 else 1024))
                prev = self.pe_bank.get(key)
                if prev is not None:
                    preds.add(prev)
                self.pe_bank[key] = o
        ch = CHAIN.get(o.eng)
        if ch is not None:
            if ch:
                if not (o.eng == "dve" and DBG_FREE[0] <= o.id < DBG_FREE[1]):
                    preds.add(ch[0])
                ch[0] = o
            else:
                ch.append(o)
        preds.discard(o)
        o.preds = preds
        for ap in reads:
            if not _onchip(ap):
                continue
            r = region_of(ap)
            self.liver.setdefault(r[0], []).append([r, "r", o])
        for ap in writes:
            if not _onchip(ap):
                continue
            r = region_of(ap)
            lst = self.live.setdefault(r[0], [])
            lst[:] = [ent for ent in lst if not _covers(r, ent[0])]
            lst.append([r, "w", o])
            lr = self.liver.get(r[0])
            if lr:
                lr[:] = [ent for ent in lr if not _covers(r, ent[0])]

    def op(self, eng, fn, reads=(), writes=(), cost=0.3, mode=0):
        o = Op()
        o.id = len(self.all)
        o.eng, o.fn, o.is_dma, o.is_out = eng, fn, False, False
        o.mode = mode
        o.occ = cost
        o.lat = cost + (0.1 if eng == "pe" else 0.25)
        if DEBUG_LINES:
            import sys as _s
            o.line = _s._getframe(2).f_lineno
        self._track(o, reads, writes)
        self.all.append(o)
        return o

    def dma(self, eng, out, in_, is_output=False, nbytes=0):
        o = Op()
        o.id = len(self.all)
        o.eng, o.is_dma, o.is_out = eng, True, is_output
        o.mode = 0
        o.fn = lambda e, out=out, in_=in_: e.dma_start(out=out, in_=in_)
        if not nbytes:
            nbytes = 4
            for d in out.shape:
                nbytes *= d
        if DEBUG_LINES:
            import sys as _s
            o.line = _s._getframe(1).f_lineno
        o.occ = 1.5 if eng == "pool" else 0.15
        o.lat = o.occ + 2.0 + nbytes / 100e3
        self._track(o, [in_], [out])
        self.all.append(o)
        return o

    def schedule(self):
        ops = self.all
        for o in ops:
            o.succs = []
        for o in ops:
            for p in o.preds:
                p.succs.append(o)
        for o in reversed(ops):
            m = 0.0
            for s in o.succs:
                if s.prio > m:
                    m = s.prio
            o.prio = m + o.lat
        order = {e: [] for e in ENGS}
        if not SCHEDULE:
            for o in ops:
                order[o.eng].append(o)
            return order
        cursor = {e: 0.0 for e in ENGS}
        avail = {e: [] for e in ENGS}
        ready_t = {}
        for o in ops:
            o.npend = len(o.preds)
            if o.npend == 0:
                ready_t[o.id] = 0.0
                heapq.heappush(avail[o.eng], (-o.prio, o.id, o))
        nleft = len(ops)
        while nleft:
            best_e = None
            for e in ENGS:
                if avail[e] and (best_e is None or cursor[e] < cursor[best_e]):
                    best_e = e
            e = best_e
            t = cursor[e]
            h = avail[e]
            pick = None
            popped = []
            earliest = None
            for _ in range(min(len(h), 24)):
                item = heapq.heappop(h)
                popped.append(item)
                rt = ready_t[item[1]]
                if rt <= t + 1e-9:
                    pick = item
                    break
                if earliest is None or rt < ready_t[earliest[1]]:
                    earliest = item
            if pick is None:
                pick = earliest
            for item in popped:
                if item is not pick:
                    heapq.heappush(h, item)
            o = pick[2]
            st = max(t, ready_t[o.id])
            o.start = st
            o.fin = st + o.lat
            cursor[e] = st + o.occ
            order[e].append(o)
            nleft -= 1
            for s in o.succs:
                s.npend -= 1
                if s.npend == 0:
                    rt = 0.0
                    for p in s.preds:
                        f = p.fin + (SEM_LAT if p.eng != s.eng else 0.0)
                        if f > rt:
                            rt = f
                    ready_t[s.id] = rt
                    heapq.heappush(avail[s.eng], (-s.prio, s.id, s))
        self.sim_time = max(o.fin for o in ops)
        return order

    def emit(self):
        nc = self.nc
        from contextlib import ExitStack

        order = self.schedule()
        self.order = order
        nep = {}
        for e in ENGS:
            cnt, ep = 0, 0
            uses = [0] * N_DMA_SEMS
            rr = 0
            for o in order[e]:
                if o.is_dma:
                    k = rr
                    rr = (rr + 1) % N_DMA_SEMS
                    o.waits = {}
                    if uses[k] > 0:
                        o.waits[("D", e, k)] = 16 * uses[k]
                    uses[k] += 1
                    o.sem, o.val = ("D", e, k), 16 * uses[k]
                else:
                    cnt += 1
                    if cnt > EPOCH:
                        cnt, ep = 1, ep + 1
                    o.sem, o.val = ("E", e, ep), cnt
                    o.waits = {}
            nep[e] = ep + 1
        for e in ENGS:
            waited = {}
            prev = None
            for o in order[e]:
                w = o.waits
                if e == "pe":
                    if prev is not None and (prev.mode != o.mode or o.mode == 2):
                        w[prev.sem] = max(w.get(prev.sem, 0), prev.val)
                    prev = o
                for p in o.preds:
                    if p.eng == "pe" and e == "pe":
                        continue
                    if w.get(p.sem, 0) < p.val:
                        w[p.sem] = p.val
                o.waits = []
                for k, v in sorted(w.items()):
                    if waited.get(k, 0) < v:
                        waited[k] = v
                        o.waits.append((k, v))

        with ExitStack() as st:
            sems = {}
            for e in ENGS:
                for ep in range(nep[e]):
                    sems[("E", e, ep)] = st.enter_context(nc.semaphore("sem_%s_%d" % (e, ep)))
            for e in ("sp", "pool"):
                for k in range(N_DMA_SEMS):
                    sems[("D", e, k)] = st.enter_context(nc.semaphore("semd_%s_%d" % (e, k)))
            block = st.enter_context(nc.Block())

            def run(engname, e):
                for o in order[engname]:
                    for k, v in o.waits:
                        e.wait_ge(sems[k], v)
                    ins = o.fn(e)
                    ins.then_inc(sems[o.sem], 16 if o.is_dma else 1)
                if engname == "sp":
                    final = {}
                    for o in self.all:
                        if o.is_out:
                            final[o.sem] = max(final.get(o.sem, 0), o.val)
                    for k, v in final.items():
                        e.wait_ge(sems[k], v)

            @block.tensor
            def _(e):
                run("pe", e)

            @block.scalar
            def _(e):
                run("act", e)

            @block.vector
            def _(e):
                run("dve", e)

            @block.gpsimd
            def _(e):
                run("pool", e)

            @block.sync
            def _(e):
                run("sp", e)

from concourse.bass_utils import run_bass_kernel_spmd

from contextlib import ExitStack

L = 4
NL_RUN = 4
DBG_NC = 8
DBG_NB = None
DBG_POST = True
D = 1024
NTOK = 2176
NPR = 2048
NSQ = 16
TS = 8
WB = 128
NV = 29
ALPHA = (2 * L) ** 0.25
C0 = 0.6065306597126334
LN_EPS = 1e-5
GN_EPS = 64e-5
V_CW, V_CB, V_BA, V_BX, V_AP, V_W0, V_A0, V_KK, V_KA, V_RK, V_GW, V_GB = 0, 4, 5, 6, 7, 8, 9, 10, 11, 12, 13, 14
V_L1G, V_L1B, V_L2G, V_L2B, V_MUR, V_MUL, V_CP, V_OMKA = 15, 16, 17, 18, 19, 22, 23, 24
V_NBA, V_NBX, V_NW0, V_NA0 = 25, 26, 27, 28

C_BONE, C_BONE64, C_ONESD, C_RMP, C_RMS = [i * 128 for i in range(5)]
NCF = 5 * 128
C_ID, C_SU, C_UI, C_SL, C_SSU, C_SUI, C_SSL = [i * 128 for i in range(7)]
C_SMF = 7 * 128
C_SMT = C_SMF + 16 * 128
NCB = C_SMT + 16


def make_consts():
    c = np.zeros((128, NCF), np.float32)
    d = np.zeros((128, NCB), np.float32)
    i = np.arange(128)
    s, t = i[:, None], i[None, :]
    d[:, C_ID:C_ID + 128] = (s == t)
    d[:, C_SU:C_SU + 128] = (s < t)
    d[:, C_UI:C_UI + 128] = (s <= t)
    d[:, C_SL:C_SL + 128] = (s > t)
    same = (s // 8) == (t // 8)
    d[:, C_SSU:C_SSU + 128] = (s < t) & same
    d[:, C_SUI:C_SUI + 128] = (s <= t) & same
    d[:, C_SSL:C_SSL + 128] = (s > t) & same
    c[:, C_BONE:C_BONE + 128] = ((s // 64) == (t // 64))
    c[:, C_BONE64:C_BONE64 + 128] = ((s // 64) == (t // 64)) / 64.0
    c[:, C_ONESD:C_ONESD + 128] = 1.0 / D
    c[:, C_RMP:C_RMP + 128] = (t % 128 != 0)
    c[:, C_RMS:C_RMS + 128] = (t % 8 != 0)
    for j in range(16):
        d[:, C_SMF + j * 128:C_SMF + (j + 1) * 128] = ((t // 8) == j)
        d[:, C_SMT + j] = ((i // 8) == j)
    return c, d


def build_program():
    nc = bass.Bass("TRN2", target_bir_lowering=False)

    def din(name, shape):
        return nc.dram_tensor(name, shape, F32, kind="ExternalInput").ap()

    def dout(name, shape):
        return nc.dram_tensor(name, shape, F32, kind="ExternalOutput").ap()

    xT = din("xT", [128, 8, NTOK])
    vecs = din("vecs", [128, L * NV * 8])
    cst = din("cst", [128, NCF])
    cstb = din("cstb", [128, NCB])
    wc = din("wc", [L, 8, 128, 8 * 896])
    wl = din("wl", [L, 128, 8 * 256])
    wo = din("wo", [L, 8, 128, 1024])
    smallw = din("smallw", [L, 8, 128, 512])
    w1 = din("w1", [L, 8, 128, 8 * 512])
    w2 = din("w2", [L, 8, 128, 4 * 1024])
    sconv = din("sconv", [L, 8, 128, 48])
    slru = din("slru", [L, 128, 128])
    sshift = din("sshift", [L, 128, 26 * 16])
    swkv = din("swkv", [L, 8, 128, 1024])
    yT = dout("yT", [128, 8, NTOK])
    o_conv_p = dout("o_conv_p", [L, 8, 128, 3])
    o_lru_p = dout("o_lru_p", [L, 128, 8])
    o_shift_p = dout("o_shift_p", [L, 128, 26])
    o_wkv_p = dout("o_wkv_p", [L, 8, 128, 64])
    o_conv_s = dout("o_conv_s", [L, 8, 128, 48])
    o_lru_s = dout("o_lru_s", [L, 128, 128])
    o_shift_s = dout("o_shift_s", [L, 128, 26 * 16])
    o_wkv_s = dout("o_wkv_s", [L, 8, 128, 1024])

    st = ExitStack()
    P = Prog(nc)
    tot = [0]

    def sb(name, cols, dt=F32):
        tot[0] += cols * (4 if dt == F32 else 2)
        return st.enter_context(nc.sbuf_tensor(name, [128, cols], dt))

    xf = st.enter_context(nc.sbuf_tensor("xf", [128, 8, NTOK], F32)); tot[0] += 8 * NTOK * 4
    xb = st.enter_context(nc.sbuf_tensor("xb", [128, 8, NTOK], BF16)); tot[0] += 8 * NTOK * 2
    vec = sb("vec", L * NV * 8)
    cf = sb("cf", NCF)
    cb = sb("cb", NCB, BF16)
    loraA = sb("loraA", NTOK, BF16)
    loraB = sb("loraB", NTOK, BF16)
    smw = [sb("smw%d" % i, 512, BF16) for i in range(2)]
    wob = [sb("wob%d" % i, 1024, BF16) for i in range(2)]
    AR = sb("AR", 16384, BF16)
    wcb = [AR[:, i * 7168:(i + 1) * 7168] for i in range(2)]
    wlb = AR[:, 14336:16384]
    w1b = [AR[:, i * 8192:i * 8192 + 4096] for i in range(2)]
    w2b = [AR[:, i * 8192 + 4096:(i + 1) * 8192] for i in range(2)]
    ps = st.enter_context(nc.psum_tensor("ps", [128, 7, 512], F32))
    pst = st.enter_context(nc.psum_tensor("pst", [128, 1024], BF16))
    psrr = [0]

    role = ["all"]
    rr_a = [0]
    rr_b = [0]

    def bank():
        if role[0] == "a":
            b = rr_a[0]
            rr_a[0] = (b + 1) % 3
        elif role[0] == "b":
            b = 3 + rr_b[0]
            rr_b[0] = (rr_b[0] + 1) % 4
        else:
            b = psrr[0]
            psrr[0] = (b + 1) % 7
        return ps[:, b, :]

    def nfree(ap):
        n = 1
        for d in ap.shape[1:]:
            n *= d
        return n

    def inps(ap):
        return str(ap.space) == "PSUM"

    def mm(out, lhsT, rhs, start=True, stop=True):
        c = max(64, nfree(rhs)) / 1400.0 * (4 if lhsT.dtype == F32 else 1) + 0.02
        P.op("pe", lambda e: e.matmul(out, lhsT, rhs, start=start, stop=stop), reads=[lhsT, rhs], writes=[out], cost=c,
             mode=(1 if lhsT.dtype == F32 else (0 if lhsT.shape[0] == 128 else 10 + region_of(lhsT)[1] // 32)))

    def tr(out, in_):
        ident = cb[0:in_.shape[0], C_ID:C_ID + in_.shape[0]]
        P.op("pe", lambda e: e.transpose(out, in_, ident), reads=[in_, ident], writes=[out], cost=0.15, mode=2)

    def act(out, in_, func, bias=None, scale=None):
        kw = {}
        rd = [in_]
        c = 0.22 + nfree(in_) / 1400.0
        if bias is not None:
            kw["bias"] = bias
            if not isinstance(bias, (int, float)):
                rd.append(bias)
                c += 0.09
        if scale is not None:
            kw["scale"] = scale
            if not isinstance(scale, (int, float)):
                rd.append(scale)
                c += 0.09
        P.op("act", lambda e: e.activation(out=out, in_=in_, func=func, **kw), reads=rd, writes=[out], cost=c)

    def ecost(eng, n, two=False):
        if eng == "pool":
            return 0.15 + n / 350.0
        return 0.09 + n * (2 if two else 1) / 960.0

    def tt(eng, out, in0, in1, op):
        two = not (inps(in0) or inps(in1))
        P.op(eng, lambda e: e.tensor_tensor(out=out, in0=in0, in1=in1, op=op), reads=[in0, in1], writes=[out], cost=ecost(eng, nfree(out), two))

    def ts(eng, out, in0, s1, op0, s2=None, op1=None):
        rd = [in0] + [s for s in (s1, s2) if s is not None and not isinstance(s, (int, float))]
        c = ecost(eng, nfree(out))
        if op1 is None:
            P.op(eng, lambda e: e.tensor_scalar(out=out, in0=in0, scalar1=s1, scalar2=None, op0=op0), reads=rd, writes=[out], cost=c)
        else:
            P.op(eng, lambda e: e.tensor_scalar(out=out, in0=in0, scalar1=s1, scalar2=s2, op0=op0, op1=op1), reads=rd, writes=[out], cost=c)

    def stt(out, in0, scalar, in1, op0, op1):
        rd = [in0, in1] + ([] if isinstance(scalar, (int, float)) else [scalar])
        P.op("dve", lambda e: e.scalar_tensor_tensor(out=out, in0=in0, scalar=scalar, in1=in1, op0=op0, op1=op1), reads=rd, writes=[out], cost=ecost("dve", nfree(out), True))

    def scan(out, d0, d1, init):
        rd = [d0, d1] + ([] if isinstance(init, (int, float)) else [init])
        P.op("dve", lambda e: e.tensor_tensor_scan(out=out, data0=d0, data1=d1, initial=init, op0=ALU.mult, op1=ALU.add), reads=rd, writes=[out], cost=ecost("dve", nfree(out), True))

    def cp(eng, out, in_):
        if eng == "act":
            P.op("act", lambda e: e.copy(out=out, in_=in_), reads=[in_], writes=[out], cost=0.22 + nfree(out) / 1400.0)
        else:
            P.op(eng, lambda e: e.tensor_copy(out=out, in_=in_), reads=[in_], writes=[out], cost=ecost(eng, nfree(out)))

    def mset(eng, ap, val):
        P.op(eng, lambda e: e.memset(ap, val), reads=[], writes=[ap], cost=ecost(eng, nfree(ap)))

    def recip(out, in_):
        P.op("dve", lambda e: e.reciprocal(out=out, in_=in_), reads=[in_], writes=[out], cost=ecost("dve", nfree(out)))

    def sigm(out, in_, nbias=None):
        if nbias is None:
            act(out, in_, AF.Exp, scale=-1.0)
        else:
            act(out, in_, AF.Exp, scale=-1.0, bias=nbias)
        ts("dve", out, out, 1.0, ALU.add)
        recip(out, out)

    def VV(l, v, c):
        o = (l * NV + v) * 8 + c
        return vec[:, o:o + 1]

    P.dma("sp", cf[:], cst)
    P.dma("sp", vec[:], vecs)
    P.dma("pool", cb[:], cstb)
    for kc in range(8):
        P.dma("sp", xf[:, kc, :], xT[:, kc, :])
        P.dma("pool", xb[:, kc, :], xT[:, kc, :])
    vec4 = vec[:].rearrange("p (l v c) -> p l v c", l=L, v=NV)
    for l in range(L):
        act(vec4[:, l, V_CP, :], vec4[:, l, V_AP, :], AF.Exp)
        act(vec4[:, l, V_CP, :], vec4[:, l, V_CP, :], AF.Ln, bias=1.0)
        ts("dve", vec4[:, l, V_CP, :], vec4[:, l, V_CP, :], -8.0, ALU.mult)
        ts("dve", vec4[:, l, V_OMKA, :], vec4[:, l, V_KA, :], -1.0, ALU.mult, 1.0, ALU.add)
        for vs, vd in ((V_BA, V_NBA), (V_BX, V_NBX), (V_W0, V_NW0), (V_A0, V_NA0)):
            ts("dve", vec4[:, l, vd, :], vec4[:, l, vs, :], -1.0, ALU.mult)

    W = WB
    lx = sb("lx", 3 + W)
    lxs = sb("lxs", 16 * 11)
    u = sb("u", W); ubf = sb("ubf", W, BF16)
    gr = sb("gr", W); gi = sb("gi", W); mlt = sb("mlt", W); hh = sb("hh", W); hc = sb("hc", 1)
    h0s = sb("h0s", 16)
    gy = sb("gy", W); sga = sb("sga", W); oa = sb("oa", W)
    shp = [sb("shp%d" % q, 1 + W) for q in range(3)]
    shs = [sb("shs%d" % q, 16 * 9) for q in range(3)]
    shlp = shp
    shls = shs
    tmp = sb("tmp", W)
    mx = [sb("mx%d" % q, W) for q in range(3)]
    vbf = sb("vbf", W, BF16)
    sg = sb("sg", W); asig = sb("asig", W); gg = sb("gg", W)
    UF = sb("UF", 1024)
    kk, kk2, rn, bb, kh, cum, gam, ig = [UF[:, i * 128:(i + 1) * 128] for i in range(8)]
    e0 = sb("e0", W)
    Rt = sb("Rt", W, BF16); At = sb("At", W, BF16); Kt = sb("Kt", W, BF16); Bt = sb("Bt", W, BF16)
    Bh = sb("Bh", 128, BF16); Kh = sb("Kh", 128, BF16)
    rk = sb("rk", W)
    TT = sb("TT", 384, BF16)
    MX = sb("MX", 512, BF16); MY = sb("MY", 512, BF16); MZ = sb("MZ", 256, BF16)
    PP = [sb("PP%d" % i, 512, BF16) for i in range(2)]
    NN = [sb("NN%d" % i, 256, BF16) for i in range(2)]
    XTs = sb("XTs", 128, BF16); UTs = sb("UTs", 128, BF16)
    Sp = sb("Sp", 64); Spb = sb("Spb", 64, BF16)
    Ss = sb("Ss", 1024); Ssb = sb("Ssb", 1024, BF16)
    UB = sb("UB", 4096, BF16)
    Atj, Rtj, BhTj, KhTj = [UB[:, i * 1024:(i + 1) * 1024] for i in range(4)]
    WK0 = dict(MX=MX, MY=MY, MZ=MZ, PP=PP, NN=NN, TT=TT, XTs=XTs, UTs=UTs, Bh=Bh, Kh=Kh)
    WK1 = dict(MX=UB[:, 0:512], MY=UB[:, 512:1024], MZ=UB[:, 1024:1280],
               PP=[UB[:, 1280:1792], UB[:, 1792:2304]], NN=[UB[:, 2304:2560], UB[:, 2560:2816]],
               TT=UB[:, 2816:3200], XTs=UB[:, 3200:3328], UTs=UB[:, 3328:3456],
               Bh=UB[:, 3456:3584], Kh=UB[:, 3584:3712])
    of_ = sb("of", W); o2 = sb("o2", W); mus = sb("mus", W); var = sb("var", W)
    mrg = sb("mrg", W, BF16)
    st_shift_p = sb("st_shift_p", 26)
    st_lru_p = sb("st_lru_p", 8)
    scv = sb("scv", 48); scvo = sb("scvo", 48); sshl = sb("sshl", 416); slrl = sb("slrl", 128)
    st_shift_s = sshl
    st_lru_s = slrl
    oa2 = [oa, sb("oa_b", W)]; rk2 = [rk, sb("rk_b", W)]; gg2 = [gg, sb("gg_b", W)]
    sgb = sb("sgb", W); tmp2 = tmp
    gendc = sb("gendc", 2); gends = sb("gends", 16)
    lnq, lnm, lnr, hr = [UF[:, i * 256:(i + 1) * 256] for i in range(4)]
    hid = [UB[:, j * 256:(j + 1) * 256] for j in range(4)]
    print("SBUF bytes/partition:", tot[0])

    blocks = [(t0, W, "p") for t0 in range(0, NPR, W)] + [(NPR, 128, "s")]
    nblk = len(blocks)

    def v3(ap, ns):
        return ap.rearrange("p (s t) -> p s t", s=ns)

    def mixer_block(l, c, bi):
        t0, Wd, kind = blocks[bi]
        ns = 1 if kind == "p" else NSQ
        T = Wd // ns
        nch = Wd // 128
        last_p = (kind == "p" and bi == nblk - 2)
        wcc = wcb[c % 2]
        xbs = lambda kc: xb[:, kc, t0:t0 + Wd]
        oa, rk, gg = oa2[bi % 2], rk2[bi % 2], gg2[bi % 2]
        WK = WK1 if (kind == "p" and bi % 2 == 1) else WK0
        MX, MY, MZ, PP, NN, TT, XTs, UTs, Bh, Kh = (WK[k] for k in ("MX", "MY", "MZ", "PP", "NN", "TT", "XTs", "UTs", "Bh", "Kh"))

        def proj(j):
            b = bank()
            for kc in range(8):
                mm(b[:, 0:Wd], wcc[:, kc * 896 + j * 128:kc * 896 + (j + 1) * 128], xbs(kc), start=(kc == 0), stop=(kc == 7))
            return b[:, 0:Wd]

        role[0] = "a"
        p_lx = proj(0)
        if kind == "p":
            L3 = v3(lx[:, 0:3 + T], 1)
            if bi == 0:
                mset("pool", lx[:, 0:3], 0.0)
            else:
                Tp = blocks[bi - 1][1]
                cp("pool", tmp[:, 0:3], lx[:, Tp:Tp + 3])
                cp("pool", lx[:, 0:3], tmp[:, 0:3])
        else:
            L3 = v3(lxs[:, :], NSQ)
        cp("act", L3[:, :, 3:3 + T], v3(p_lx, ns))
        u3 = v3(u[:, 0:Wd], ns)
        ts("dve", u3, L3[:, :, 0:T], VV(l, V_CW + 0, c), ALU.mult, VV(l, V_CB, c), ALU.add)
        for j in range(1, 4):
            stt(u3, L3[:, :, j:j + T], VV(l, V_CW + j, c), u3, ALU.mult, ALU.add)
        if last_p:
            P.dma("sp", o_conv_p[l, c], lx[:, T:T + 3], is_output=True)
        if kind == "s":
            cp("pool", v3(scvo[:, :], NSQ), L3[:, :, 8:11])
            P.dma("sp", o_conv_s[l, c], scvo[:, :], is_output=True)
        cp("act", ubf[:, 0:Wd], u[:, 0:Wd])
        p_gr = bank()[:, 0:Wd]
        smc = smw[c % 2]
        mm(p_gr, smc[:, 256:384], ubf[:, 0:Wd])
        p_gi = bank()[:, 0:Wd]
        mm(p_gi, smc[:, 384:512], ubf[:, 0:Wd])
        sigm(gr[:, 0:Wd], p_gr, VV(l, V_NBA, c))
        sigm(gi[:, 0:Wd], p_gi, VV(l, V_NBX, c))
        act(gr[:, 0:Wd], gr[:, 0:Wd], AF.Exp, scale=VV(l, V_CP, c))
        act(mlt[:, 0:Wd], gr[:, 0:Wd], AF.Square)
        act(mlt[:, 0:Wd], mlt[:, 0:Wd], AF.Ln, bias=1.0, scale=-1.0)
        act(mlt[:, 0:Wd], mlt[:, 0:Wd], AF.Exp, scale=0.5)
        if kind == "p" and bi == 0:
            mset("pool", mlt[:, 0:1], 1.0)
        tt("dve", gi[:, 0:Wd], gi[:, 0:Wd], u[:, 0:Wd], ALU.mult)
        tt("dve", gi[:, 0:Wd], gi[:, 0:Wd], mlt[:, 0:Wd], ALU.mult)
        if kind == "p":
            if bi == 0:
                scan(hh[:, 0:Wd], gr[:, 0:Wd], gi[:, 0:Wd], 0.0)
            else:
                scan(hh[:, 0:Wd], gr[:, 0:Wd], gi[:, 0:Wd], hc[:, 0:1])
            cp("pool", hc[:, 0:1], hh[:, Wd - 1:Wd])
            if last_p:
                cp("pool", st_lru_p[:, c:c + 1], hh[:, Wd - 1:Wd])
        else:
            for j in range(NSQ):
                scan(hh[:, j * 8:(j + 1) * 8], gr[:, j * 8:(j + 1) * 8], gi[:, j * 8:(j + 1) * 8], h0s[:, j:j + 1])
            cp("pool", st_lru_s[:, c * 16:(c + 1) * 16].unsqueeze(2), v3(hh[:, 0:Wd], NSQ)[:, :, 7:8])
        p_ly = proj(1)
        act(gy[:, 0:Wd], p_ly, AF.Gelu_apprx_tanh)
        p_ga = proj(5)
        sigm(sga[:, 0:Wd], p_ga)
        tt("dve", oa[:, 0:Wd], hh[:, 0:Wd], gy[:, 0:Wd], ALU.mult)
        tt("pool", oa[:, 0:Wd], oa[:, 0:Wd], sga[:, 0:Wd], ALU.mult)

        for q in range(3):
            pq = proj(2 + q)
            if kind == "p":
                S3 = v3(shp[q][:, 0:1 + T], 1)
                if bi == 0:
                    mset("pool", shp[q][:, 0:1], 0.0)
                else:
                    Tp = blocks[bi - 1][1]
                    cp("pool", shp[q][:, 0:1], shp[q][:, Tp:Tp + 1])
            else:
                S3 = v3(shs[q][:, :], NSQ)
            cp("act", S3[:, :, 1:1 + T], v3(pq, ns))
            t3 = v3(tmp2[:, 0:Wd], ns)
            tt("pool", t3, S3[:, :, 0:T], S3[:, :, 1:1 + T], ALU.subtract)
            stt(v3(mx[q][:, 0:Wd], ns), t3, VV(l, V_MUR + q, c), S3[:, :, 1:1 + T], ALU.mult, ALU.add)
            if last_p:
                cp("pool", st_shift_p[:, q * 8 + c:q * 8 + c + 1], shp[q][:, T:T + 1])
            if kind == "s":
                o = (q * 8 + c) * 16
                cp("pool", st_shift_s[:, o:o + 16].unsqueeze(2), S3[:, :, 8:9])
        r_m, k_m, v_m = mx[0][:, 0:Wd], mx[1][:, 0:Wd], mx[2][:, 0:Wd]
        cs = slice(c * 128, (c + 1) * 128)
        p_d = bank()[:, 0:Wd]
        mm(p_d, smc[0:64, 0:128], loraA[0:64, t0:t0 + Wd])
        p_a = bank()[:, 0:Wd]
        mm(p_a, smc[64:128, 0:128], loraA[64:128, t0:t0 + Wd])
        p_g = bank()[:, 0:Wd]
        mm(p_g, smc[:, 128:256], loraB[:, t0:t0 + Wd])
        sigm(sg[:, 0:Wd], p_d, VV(l, V_NW0, c))
        sigm(asig[:, 0:Wd], p_a, VV(l, V_NA0, c))
        cp("act", gg[:, 0:Wd], p_g)
        act(kk[:, 0:Wd], k_m, AF.Identity, scale=VV(l, V_KK, c))
        act(kk2[:, 0:Wd], kk[:, 0:Wd], AF.Square)
        p_n = bank()[:, 0:Wd]
        mm(p_n, cf[:, C_BONE:C_BONE + 128], kk2[:, 0:Wd])
        ts("dve", rn[:, 0:Wd], p_n, 1e-24, ALU.max)
        act(rn[:, 0:Wd], rn[:, 0:Wd], AF.Ln)
        act(rn[:, 0:Wd], rn[:, 0:Wd], AF.Exp, scale=-0.5)
        tt("dve", kk[:, 0:Wd], kk[:, 0:Wd], rn[:, 0:Wd], ALU.mult)
        tt("dve", bb[:, 0:Wd], kk[:, 0:Wd], asig[:, 0:Wd], ALU.mult)
        ts("dve", kh[:, 0:Wd], asig[:, 0:Wd], VV(l, V_KA, c), ALU.mult, VV(l, V_OMKA, c), ALU.add)
        tt("dve", kh[:, 0:Wd], kh[:, 0:Wd], k_m, ALU.mult)
        rmask = cf[:, C_RMS:C_RMS + 128] if kind == "s" else None
        if kind == "p":
            for q in range(nch):
                scan(cum[:, q * 128:(q + 1) * 128], cf[:, C_RMP:C_RMP + 128], sg[:, q * 128:(q + 1) * 128], 0.0)
        else:
            scan(cum[:, 0:Wd], rmask, sg[:, 0:Wd], 0.0)
        act(gam[:, 0:Wd], cum[:, 0:Wd], AF.Exp, scale=-C0)
        act(ig[:, 0:Wd], cum[:, 0:Wd], AF.Exp, scale=C0)
        tt("pool", tmp[:, 0:Wd], cum[:, 0:Wd], sg[:, 0:Wd], ALU.subtract)
        act(e0[:, 0:Wd], tmp[:, 0:Wd], AF.Exp, scale=-C0)
        tt("dve", Rt[:, 0:Wd], r_m, gam[:, 0:Wd], ALU.mult)
        stt(At[:, 0:Wd], kk[:, 0:Wd], -1.0, e0[:, 0:Wd], ALU.mult, ALU.mult)
        tt("pool", Kt[:, 0:Wd], kh[:, 0:Wd], ig[:, 0:Wd], ALU.mult)
        tt("pool", Bt[:, 0:Wd], bb[:, 0:Wd], ig[:, 0:Wd], ALU.mult)
        stt(rk[:, 0:Wd], r_m, VV(l, V_RK, c), kh[:, 0:Wd], ALU.mult, ALU.mult)
        p_bon = bank()[:, 0:Wd]
        mm(p_bon, cf[:, C_BONE:C_BONE + 128], rk[:, 0:Wd])
        tt("dve", rk[:, 0:Wd], p_bon, v_m, ALU.mult)
        cp("act", vbf[:, 0:Wd], v_m)

        role[0] = "b"
        S_f = Sp if kind == "p" else Ss
        S_b = Spb if kind == "p" else Ssb
        msu, mui, msl = (C_SU, C_UI, C_SL) if kind == "p" else (C_SSU, C_SUI, C_SSL)
        for q in range(nch):
            ck = slice(q * 128, (q + 1) * 128)
            first = (kind == "p" and bi == 0 and q == 0)
            if kind == "p":
                gend = gendc[:, bi % 2:bi % 2 + 1]
                cp("pool", gend, gam[:, q * 128 + 127:q * 128 + 128])
                ts("dve", Bh[:, :], Bt[:, ck], gend, ALU.mult)
                ts("dve", Kh[:, :], Kt[:, ck], gend, ALU.mult)
            else:
                cp("pool", gends[:, :].unsqueeze(2), v3(gam[:, 0:128], NSQ)[:, :, 7:8])
                g3 = gends[:, :].unsqueeze(2)
                tt("dve", v3(Bh[:, :], NSQ), v3(Bt[:, ck], NSQ), g3.to_broadcast([128, NSQ, 8]), ALU.mult)
                tt("dve", v3(Kh[:, :], NSQ), v3(Kt[:, ck], NSQ), g3.to_broadcast([128, NSQ, 8]), ALU.mult)
            tr(pst[:, 0:128], vbf[:, ck])
            tr(pst[:, 128:256], Bh[:, :])
            tr(pst[:, 256:384], Kh[:, :])
            cp("act", TT[:, 0:384], pst[:, 0:384])
            vT, BhT, KhT = TT[:, 0:128], TT[:, 128:256], TT[:, 256:384]
            bX = bank(); bY = bank(); bZ = bank()
            for hp in range(2):
                hs = slice(hp * 64, (hp + 1) * 64)
                mm(bX[:, hp * 128:(hp + 1) * 128], Bt[hs, ck], At[hs, ck])
                mm(bX[:, (2 + hp) * 128:(3 + hp) * 128], Kt[hs, ck], At[hs, ck])
                mm(bY[:, hp * 128:(hp + 1) * 128], Bt[hs, ck], Rt[hs, ck])
                mm(bY[:, (2 + hp) * 128:(3 + hp) * 128], Kt[hs, ck], Rt[hs, ck])
                mm(bZ[:, hp * 128:(hp + 1) * 128], At[hs, ck], Bt[hs, ck])
            m4 = lambda o: cb[:, o:o + 128].unsqueeze(1).to_broadcast([128, 4, 128])
            m2 = lambda o: cb[:, o:o + 128].unsqueeze(1).to_broadcast([128, 2, 128])
            r4 = lambda a: a.rearrange("p (a b) -> p a b", b=128)
            tt("dve", r4(MX[:, :]), r4(bX[:, 0:512]), m4(msu), ALU.mult)
            tt("dve", r4(MY[:, :]), r4(bY[:, 0:512]), m4(mui), ALU.mult)
            tt("dve", r4(MZ[:, :]), r4(bZ[:, 0:256]), m2(msl), ALU.mult)
            nlev = 6 if kind == "p" else 2
            cur = 0
            cp("pool", PP[0][:, 0:256], MX[:, 0:256])
            cp("pool", PP[0][:, 256:512], MZ[:, 0:256])
            tt("pool", r4(NN[0][:, :]), r4(MX[:, 0:256]), m2(C_ID), ALU.add)
            ncur = 0
            for i in range(nlev):
                Pc = PP[cur]
                bQ = bank()
                for hp in range(2):
                    hcs = slice(hp * 128, (hp + 1) * 128)
                    hts = slice(256 + hp * 128, 256 + (hp + 1) * 128)
                    if i < nlev - 1:
                        mm(bQ[:, hcs], Pc[:, hts], Pc[:, hcs])
                    mm(bQ[:, hts], Pc[:, hcs], Pc[:, hts])
                if i >= 1:
                    bR = bank()
                    for hp in range(2):
                        hcs = slice(hp * 128, (hp + 1) * 128)
                        hts = slice(256 + hp * 128, 256 + (hp + 1) * 128)
                        mm(bR[:, hcs], Pc[:, hts], NN[ncur][:, hcs])
                    tt("dve", NN[1 - ncur][:, :], bR[:, 0:256], NN[ncur][:, :], ALU.add)
                    ncur = 1 - ncur
                nxt = 1 - cur
                if i < nlev - 1:
                    cp("act", PP[nxt][:, :], bQ[:, 0:512])
                else:
                    cp("act", PP[nxt][:, 256:512], bQ[:, 256:512])
                cur = nxt
            bR = bank()
            for hp in range(2):
                hcs = slice(hp * 128, (hp + 1) * 128)
                hts = slice(256 + hp * 128, 256 + (hp + 1) * 128)
                mm(bR[:, hcs], PP[cur][:, hts], NN[ncur][:, hcs])
            tt("dve", NN[1 - ncur][:, :], bR[:, 0:256], NN[ncur][:, :], ALU.add)
            ncur = 1 - ncur
            Nf = NN[ncur]
            if kind == "s":
                b3 = lambda a: a.unsqueeze(1).to_broadcast([128, 8, 128])
                j3 = lambda a: a.rearrange("p (j s) -> p j s", j=8)

                def smf(half):
                    return cb[:, C_SMF + half * 1024:C_SMF + (half + 1) * 1024].rearrange("p (j s) -> p j s", j=8)

                def smt(half):
                    return cb[:, C_SMT + half * 8:C_SMT + (half + 1) * 8].unsqueeze(2).to_broadcast([128, 8, 128])
            bXT = bank()
            bXT2 = [bXT, bank()] if kind == "s" else None
            for hp in range(2):
                hs = slice(hp * 64, (hp + 1) * 64)
                o = bXT[:, hp * 64:(hp + 1) * 64] if kind == "p" else bXT2[hp][:, 0:64]
                if kind == "p":
                    if not first:
                        mm(o, At[hs, ck], S_b[hs, 0:64], start=True, stop=False)
                mm(o, MX[:, (2 + hp) * 128:(3 + hp) * 128], vT[:, hs], start=(first or kind == "s"), stop=(kind == "p"))
            if kind == "s":
                for half in range(2):
                    tt("pool", j3(Atj[:, :]), b3(At[:, ck]), smf(half), ALU.mult)
                    for hp in range(2):
                        hs = slice(hp * 64, (hp + 1) * 64)
                        o = bXT2[hp][:, 0:64]
                        for jj in range(8):
                            j = half * 8 + jj
                            mm(o, Atj[hs, jj * 128:(jj + 1) * 128], S_b[hs, j * 64:(j + 1) * 64], start=False, stop=(j == NSQ - 1))
            if kind == "p":
                cp("act", XTs[:, :], bXT[:, 0:128])
            else:
                cp("act", XTs[:, 0:64], bXT2[0][:, 0:64])
                cp("act", XTs[:, 64:128], bXT2[1][:, 0:64])
            bUT = bank()
            for hp in range(2):
                mm(bUT[:, hp * 64:(hp + 1) * 64], Nf[:, hp * 128:(hp + 1) * 128], XTs[:, hp * 64:(hp + 1) * 64])
            cp("act", UTs[:, :], bUT[:, 0:128])
            bO = bank()
            for hp in range(2):
                hs = slice(hp * 64, (hp + 1) * 64)
                o = bO[hs, 0:128]
                if kind == "p":
                    if not first:
                        mm(o, S_b[hs, 0:64], Rt[hs, ck], start=True, stop=False)
                mm(o, UTs[:, hs], MY[:, hp * 128:(hp + 1) * 128], start=(first or kind == "s"), stop=False)
                mm(o, vT[:, hs], MY[:, (2 + hp) * 128:(3 + hp) * 128], start=False, stop=(kind == "p"))
            if kind == "s":
                for half in range(2):
                    tt("pool", j3(Rtj[:, :]), b3(Rt[:, ck]), smf(half), ALU.mult)
                    for hp in range(2):
                        hs = slice(hp * 64, (hp + 1) * 64)
                        o = bO[hs, 0:128]
                        for jj in range(8):
                            j = half * 8 + jj
                            mm(o, S_b[hs, j * 64:(j + 1) * 64], Rtj[hs, jj * 128:(jj + 1) * 128], start=False, stop=(j == NSQ - 1))
            cp("act", of_[:, ck], bO[:, 0:128])
            if kind == "p":
                bS = bank()
                for hp in range(2):
                    hs = slice(hp * 64, (hp + 1) * 64)
                    mm(bS[hs, 0:64], BhT[:, hs], UTs[:, hs], start=True, stop=False)
                    mm(bS[hs, 0:64], KhT[:, hs], vT[:, hs], start=False, stop=True)
                if first:
                    cp("dve", S_f[:, :], bS[:, 0:64])
                else:
                    stt(S_f[:, :], S_f[:, :], gend, bS[:, 0:64], ALU.mult, ALU.add)
                cp("act", S_b[:, :], S_f[:, :])
            else:
                gb = gends[:, :].unsqueeze(2).to_broadcast([128, NSQ, 64])
                tt("dve", v3(S_f[:, :], NSQ), v3(S_f[:, :], NSQ), gb, ALU.mult)
                for half in range(2):
                    tt("pool", j3(BhTj[:, :]), b3(BhT), smt(half), ALU.mult)
                    tt("pool", j3(KhTj[:, :]), b3(KhT), smt(half), ALU.mult)
                    bS = bank()
                    for hp in range(2):
                        hs = slice(hp * 64, (hp + 1) * 64)
                        for jj in range(8):
                            mm(bS[hs, jj * 64:(jj + 1) * 64], BhTj[:, jj * 128 + hp * 64:jj * 128 + (hp + 1) * 64], UTs[:, hs], start=True, stop=False)
                            mm(bS[hs, jj * 64:(jj + 1) * 64], KhTj[:, jj * 128 + hp * 64:jj * 128 + (hp + 1) * 64], vT[:, hs], start=False, stop=True)
                    tt("dve", S_f[:, half * 512:(half + 1) * 512], S_f[:, half * 512:(half + 1) * 512], bS[:, 0:512], ALU.add)
        if last_p:
            P.dma("sp", o_wkv_p[l, c], Sp[:, :], is_output=True)
        if kind == "s":
            P.dma("sp", o_wkv_s[l, c], Ss[:, :], is_output=True)

        act(o2[:, 0:Wd], of_[:, 0:Wd], AF.Square)
        p_mu = bank()[:, 0:Wd]
        mm(p_mu, cf[:, C_BONE64:C_BONE64 + 128], of_[:, 0:Wd])
        p_m2 = bank()[:, 0:Wd]
        mm(p_m2, cf[:, C_BONE64:C_BONE64 + 128], o2[:, 0:Wd])
        cp("act", mus[:, 0:Wd], p_mu)
        act(var[:, 0:Wd], p_mu, AF.Square)
        tt("dve", var[:, 0:Wd], p_m2, var[:, 0:Wd], ALU.subtract)
        act(var[:, 0:Wd], var[:, 0:Wd], AF.Ln, bias=cf_eps_gn[:, 0:1])
        act(var[:, 0:Wd], var[:, 0:Wd], AF.Exp, scale=-0.5)
        tt("dve", of_[:, 0:Wd], of_[:, 0:Wd], mus[:, 0:Wd], ALU.subtract)
        tt("dve", of_[:, 0:Wd], of_[:, 0:Wd], var[:, 0:Wd], ALU.mult)
        ts("dve", of_[:, 0:Wd], of_[:, 0:Wd], VV(l, V_GW, c), ALU.mult, VV(l, V_GB, c), ALU.add)
        tt("pool", of_[:, 0:Wd], of_[:, 0:Wd], rk[:, 0:Wd], ALU.add)
        tt("pool", of_[:, 0:Wd], of_[:, 0:Wd], gg[:, 0:Wd], ALU.mult)
        p_gb = proj(6)
        sigm(sgb[:, 0:Wd], p_gb)
        tt("dve", of_[:, 0:Wd], of_[:, 0:Wd], sgb[:, 0:Wd], ALU.mult)
        tt("dve", mrg[:, 0:Wd], of_[:, 0:Wd], oa[:, 0:Wd], ALU.add)
        wo_c = wob[c % 2]
        for oc in range(8):
            b = bank()[:, 0:Wd]
            mm(b, wo_c[:, oc * 128:(oc + 1) * 128], mrg[:, 0:Wd])
            xs = xf[:, oc, t0:t0 + Wd]
            if c == 0:
                stt(xs, xs, ALPHA, b, ALU.mult, ALU.add)
            else:
                tt("dve", xs, b, xs, ALU.add)
        role[0] = "all"

    cf_eps_gn = sb("epsgn", 1)
    cf_eps_ln = sb("epsln", 1)
    mset("pool", cf_eps_gn[:, :], GN_EPS)
    mset("pool", cf_eps_ln[:, :], LN_EPS)

    def layer_norm(l, vg, vb):
        for t0 in range(0, NTOK, 256):
            Wd = min(256, NTOK - t0)
            p_s = bank()[:, 0:Wd]
            for oc in range(8):
                mm(p_s, cf[:, C_ONESD:C_ONESD + 128], xf[:, oc, t0:t0 + Wd], start=(oc == 0), stop=(oc == 7))
            p_q = bank()[:, 0:Wd]
            for oc in range(8):
                act(lnq[:, 0:Wd], xf[:, oc, t0:t0 + Wd], AF.Square)
                mm(p_q, cf[:, C_ONESD:C_ONESD + 128], lnq[:, 0:Wd], start=(oc == 0), stop=(oc == 7))
            cp("act", lnm[:, 0:Wd], p_s)
            act(lnr[:, 0:Wd], p_s, AF.Square)
            tt("dve", lnr[:, 0:Wd], p_q, lnr[:, 0:Wd], ALU.subtract)
            act(lnr[:, 0:Wd], lnr[:, 0:Wd], AF.Sqrt, bias=cf_eps_ln[:, 0:1])
            recip(lnr[:, 0:Wd], lnr[:, 0:Wd])
            for oc in range(8):
                xs = xf[:, oc, t0:t0 + Wd]
                tt("dve", xs, xs, lnm[:, 0:Wd], ALU.subtract)
                tt("dve", xs, xs, lnr[:, 0:Wd], ALU.mult)
                ts("dve", xs, xs, VV(l, vg, oc), ALU.mult, VV(l, vb, oc), ALU.add)
                cp("act", xb[:, oc, t0:t0 + Wd], xs)

    for l in range(NL_RUN):
        P.dma("pool", wlb, wl[l])
        P.dma("pool", wcb[0], wc[l, 0])
        P.dma("pool", wob[0][:], wo[l, 0])
        P.dma("pool", smw[0][:], smallw[l, 0])
        P.dma("sp", sshl[:, :], sshift[l])
        P.dma("sp", slrl[:, :], slru[l])
        for q in range(2):
            cp("pool", v3(shls[q][:, :], NSQ)[:, :, 0:1], sshl[:, (24 + q) * 16:(25 + q) * 16].unsqueeze(2))
        for bi, (t0, Wd, kind) in enumerate(blocks):
            ns = 1 if kind == "p" else NSQ
            T = Wd // ns
            for q in range(2):
                b = bank()[:, 0:Wd]
                for kc in range(8):
                    mm(b, wlb[:, kc * 256 + q * 128:kc * 256 + (q + 1) * 128], xb[:, kc, t0:t0 + Wd], start=(kc == 0), stop=(kc == 7))
                if kind == "p":
                    S3 = v3(shlp[q][:, 0:1 + T], 1)
                    if bi == 0:
                        mset("pool", shlp[q][:, 0:1], 0.0)
                    else:
                        Tp = blocks[bi - 1][1]
                        cp("pool", shlp[q][:, 0:1], shlp[q][:, Tp:Tp + 1])
                else:
                    S3 = v3(shls[q][:, :], NSQ)
                cp("act", S3[:, :, 1:1 + T], v3(b, ns))
                t3 = v3(tmp[:, 0:Wd], ns)
                tt("pool", t3, S3[:, :, 0:T], S3[:, :, 1:1 + T], ALU.subtract)
                stt(v3(u[:, 0:Wd], ns), t3, VV(l, V_MUL, q), S3[:, :, 1:1 + T], ALU.mult, ALU.add)
                if q == 0:
                    act(loraA[0:64, t0:t0 + Wd], u[0:64, 0:Wd], AF.Tanh)
                    cp("act", loraA[64:128, t0:t0 + Wd], u[64:128, 0:Wd])
                else:
                    act(loraB[:, t0:t0 + Wd], u[:, 0:Wd], AF.Sigmoid)
                if kind == "p" and bi == nblk - 2:
                    cp("pool", st_shift_p[:, 24 + q:25 + q], shlp[q][:, T:T + 1])
                if kind == "s":
                    cp("pool", st_shift_s[:, (24 + q) * 16:(25 + q) * 16].unsqueeze(2), S3[:, :, 8:9])
        for c in range(DBG_NC):
            if c + 1 < DBG_NC:
                P.dma("pool", wcb[(c + 1) % 2], wc[l, c + 1])
                P.dma("pool", wob[(c + 1) % 2][:], wo[l, c + 1])
                P.dma("pool", smw[(c + 1) % 2][:], smallw[l, c + 1])
            P.dma("sp", scv[:, :], sconv[l, c])
            cp("pool", v3(lxs[:, :], NSQ)[:, :, 0:3], v3(scv[:, :], NSQ))
            cp("pool", h0s[:, :], slrl[:, c * 16:(c + 1) * 16])
            for q in range(3):
                o = (q * 8 + c) * 16
                cp("pool", v3(shs[q][:, :], NSQ)[:, :, 0:1], sshl[:, o:o + 16].unsqueeze(2))
            P.dma("sp", Ss[:, :], swkv[l, c])
            cp("pool", Ssb[:, :], Ss[:, :])
            for bi in (range(nblk) if DBG_NB is None else DBG_NB):
                mixer_block(l, c, bi)
        P.dma("sp", o_lru_p[l], st_lru_p[:, :], is_output=True)
        P.dma("sp", o_lru_s[l], st_lru_s[:, :], is_output=True)
        P.dma("sp", o_shift_p[l], st_shift_p[:, :], is_output=True)
        P.dma("sp", o_shift_s[l], st_shift_s[:, :], is_output=True)
        if not DBG_POST:
            continue
        layer_norm(l, V_L1G, V_L1B)
        for e8 in range(8):
            P.dma("pool", w1b[e8 % 2], w1[l, e8])
            P.dma("pool", w2b[e8 % 2], w2[l, e8])
            for t0 in range(0, NTOK, 256):
                Wd = min(256, NTOK - t0)
                for jj in range(4):
                    b = bank()[:, 0:Wd]
                    for kc in range(8):
                        mm(b, w1b[e8 % 2][:, kc * 512 + jj * 128:kc * 512 + (jj + 1) * 128], xb[:, kc, t0:t0 + Wd], start=(kc == 0), stop=(kc == 7))
                    act(hr[:, 0:Wd], b, AF.Relu)
                    act(hid[jj][:, 0:Wd], hr[:, 0:Wd], AF.Square)
                for oc in range(8):
                    b = bank()[:, 0:Wd]
                    for jj in range(4):
                        mm(b, w2b[e8 % 2][:, jj * 1024 + oc * 128:jj * 1024 + (oc + 1) * 128], hid[jj][:, 0:Wd], start=(jj == 0), stop=(jj == 3))
                    xs = xf[:, oc, t0:t0 + Wd]
                    if e8 == 0:
                        stt(xs, xs, ALPHA, b, ALU.mult, ALU.add)
                    else:
                        tt("dve", xs, b, xs, ALU.add)
        layer_norm(l, V_L2G, V_L2B)
    for kc in range(8):
        P.dma("sp", yT[:, kc, :], xf[:, kc, :], is_output=True)
    P.emit()
    st.close()
    print("ops:", len(P.all), "sim_time_us:", getattr(P, "sim_time", None))
    return nc


def _prep_shared(inp):
    f = lambda k: np.asarray(inp[k], np.float32)
    w_in = f("w_in")
    V = np.zeros((L, NV, 1024), np.float32)
    V[:, 0:4] = f("conv_w")
    V[:, V_CB] = f("conv_b")
    V[:, V_BA] = f("lru_ba").reshape(L, 1024)
    V[:, V_BX] = f("lru_bx").reshape(L, 1024)
    V[:, V_AP] = f("lru_a_param")
    V[:, V_W0] = f("w0"); V[:, V_A0] = f("a0"); V[:, V_KK] = f("k_k"); V[:, V_KA] = f("k_a")
    V[:, V_RK] = f("r_k").reshape(L, 1024)
    V[:, V_GW] = f("gn_w"); V[:, V_GB] = f("gn_b")
    V[:, V_L1G] = f("ln1_g"); V[:, V_L1B] = f("ln1_b"); V[:, V_L2G] = f("ln2_g"); V[:, V_L2B] = f("ln2_b")
    mu = f("shift_mu")
    V[:, V_MUR] = mu[:, 0:1024]; V[:, V_MUR + 1] = mu[:, 1024:2048]; V[:, V_MUR + 2] = mu[:, 2048:3072]
    V[:, V_MUL, 0:256] = mu[:, 3072:3328]
    vecs = np.ascontiguousarray(V.reshape(L, NV, 8, 128).transpose(3, 0, 1, 2).reshape(128, L * NV * 8))
    offs = [0, 1024, 2048, 3072, 4096, 5376, 6400]
    wc = np.empty((L, 8, 128, 8 * 896), np.float32)
    for c in range(8):
        cols = np.concatenate([np.arange(o + c * 128, o + (c + 1) * 128) for o in offs])
        sel = w_in[:, :, cols]
        wc[:, c] = sel.reshape(L, 8, 128, 896).transpose(0, 2, 1, 3).reshape(L, 128, 8 * 896)
    wl = np.ascontiguousarray(w_in[:, :, 5120:5376].reshape(L, 8, 128, 256).transpose(0, 2, 1, 3).reshape(L, 128, 2048))
    wo = np.ascontiguousarray(f("w_out").reshape(L, 8, 128, 1024))
    upab = np.ascontiguousarray(np.concatenate([f("decay_up"), f("aaa_up")], axis=1))
    upg = np.ascontiguousarray(f("gate_up"))
    lruw = np.zeros((L, 128, 2, 8, 128), np.float32)
    for g, key in enumerate(("lru_wa", "lru_wx")):
        wg = f(key)
        for c in range(8):
            for hp in range(2):
                lruw[:, hp * 64:(hp + 1) * 64, g, c, hp * 64:(hp + 1) * 64] = wg[:, 2 * c + hp]
    smallw = np.empty((L, 8, 128, 512), np.float32)
    for c in range(8):
        smallw[:, c, :, 0:128] = upab[:, :, c * 128:(c + 1) * 128]
        smallw[:, c, :, 128:256] = upg[:, :, c * 128:(c + 1) * 128]
        smallw[:, c, :, 256:384] = lruw[:, :, 0, c, :]
        smallw[:, c, :, 384:512] = lruw[:, :, 1, c, :]
    w1 = np.ascontiguousarray(f("mlp_w1").reshape(L, 8, 128, 8, 512).transpose(0, 3, 2, 1, 4).reshape(L, 8, 128, 4096))
    w2 = np.ascontiguousarray(f("mlp_w2").reshape(L, 8, 4, 128, 1024).transpose(0, 1, 3, 2, 4).reshape(L, 8, 128, 4096))
    cc, cd = make_consts()
    return dict(vecs=vecs, cst=cc, cstb=cd, wc=wc, wl=wl, wo=wo, smallw=smallw, w1=w1, w2=w2)


_NC_CACHE = {}


def kernel(**inp):
    f = lambda k: np.asarray(inp[k], np.float32)
    shared = _prep_shared(inp)
    xp, xs = f("x_prompt"), f("x_sample")
    s_conv, s_lru, s_shift, s_wkv = f("state_conv"), f("state_lru"), f("state_shift"), f("state_wkv")
    in_maps = []
    for core in range(8):
        cs = slice(core * 16, (core + 1) * 16)
        xtok = np.concatenate([xp[core], xs[cs].reshape(128, 1024)], axis=0)
        m = dict(shared)
        m["xT"] = np.ascontiguousarray(xtok.T.reshape(8, 128, NTOK).transpose(1, 0, 2))
        m["sconv"] = np.ascontiguousarray(s_conv[:, cs].reshape(L, 16, 3, 8, 128).transpose(0, 3, 4, 1, 2).reshape(L, 8, 128, 48))
        m["slru"] = np.ascontiguousarray(s_lru[:, cs].reshape(L, 16, 8, 128).transpose(0, 3, 2, 1).reshape(L, 128, 128))
        m["sshift"] = np.ascontiguousarray(s_shift[:, cs].reshape(L, 16, 26, 128).transpose(0, 3, 2, 1).reshape(L, 128, 416))
        m["swkv"] = np.ascontiguousarray(s_wkv[:, cs].reshape(L, 16, 8, 2, 64, 64).transpose(0, 2, 3, 5, 1, 4).reshape(L, 8, 128, 1024))
        in_maps.append(m)
    if "nc" not in _NC_CACHE:
        _NC_CACHE["nc"] = build_program()
    res = run_bass_kernel_spmd(_NC_CACHE["nc"], in_maps, core_ids=list(range(8)))
    R = res.results
    y_p = np.empty((8, NPR, 1024), np.float32); y_s = np.empty((128, 8, 1024), np.float32)
    conv_p = np.empty((L, 8, 3, 1024), np.float32); lru_p = np.empty((L, 8, 1024), np.float32)
    shift_p = np.empty((L, 8, 3328), np.float32); wkv_p = np.empty((L, 8, 16, 64, 64), np.float32)
    conv_s = np.empty((L, 128, 3, 1024), np.float32); lru_s = np.empty((L, 128, 1024), np.float32)
    shift_s = np.empty((L, 128, 3328), np.float32); wkv_s = np.empty((L, 128, 16, 64, 64), np.float32)
    for core in range(8):
        r = R[core]
        cs = slice(core * 16, (core + 1) * 16)
        ytok = np.asarray(r["yT"]).reshape(128, 8, NTOK).transpose(2, 1, 0).reshape(NTOK, 1024)
        y_p[core] = ytok[:NPR]
        y_s[cs] = ytok[NPR:].reshape(16, 8, 1024)
        conv_p[:, core] = np.asarray(r["o_conv_p"]).reshape(L, 8, 128, 3).transpose(0, 3, 1, 2).reshape(L, 3, 1024)
        lru_p[:, core] = np.asarray(r["o_lru_p"]).reshape(L, 128, 8).transpose(0, 2, 1).reshape(L, 1024)
        shift_p[:, core] = np.asarray(r["o_shift_p"]).reshape(L, 128, 26).transpose(0, 2, 1).reshape(L, 3328)
        wkv_p[:, core] = np.asarray(r["o_wkv_p"]).reshape(L, 8, 2, 64, 64).transpose(0, 1, 2, 4, 3).reshape(L, 16, 64, 64)
        conv_s[:, cs] = np.asarray(r["o_conv_s"]).reshape(L, 8, 128, 16, 3).transpose(0, 3, 4, 1, 2).reshape(L, 16, 3, 1024)
        lru_s[:, cs] = np.asarray(r["o_lru_s"]).reshape(L, 128, 8, 16).transpose(0, 3, 2, 1).reshape(L, 16, 1024)
        shift_s[:, cs] = np.asarray(r["o_shift_s"]).reshape(L, 128, 26, 16).transpose(0, 3, 2, 1).reshape(L, 16, 3328)
        wkv_s[:, cs] = np.asarray(r["o_wkv_s"]).reshape(L, 8, 2, 64, 16, 64).transpose(0, 4, 1, 2, 5, 3).reshape(L, 16, 16, 64, 64)
    return (y_p, y_s, conv_p, lru_p, shift_p, wkv_p, conv_s, lru_s, shift_s, wkv_s)
```
